# Optimizing a Trainium2 kernel written in Bass

```python
import math
import jax, jax.numpy as jnp
from jax import lax
import numpy as np

D_MODEL = 2048
BATCH = 4
SEQ = 4096
DEPTH = 1

EPS = 1e-6
ROPE_THETA = 10000.0
NEG = -1e30
Q_BLOCK = 128

MIX_WIDTH = D_MODEL
MLA_NOPE = 128
MLA_ROPE = 64
MLA_V = 128
MLA_WIDTH = MIX_WIDTH // 2
MLA_HEADS = MLA_WIDTH // MLA_V
MLA_Q_RANK = 768
MLA_KV_RANK = 512
MLA_QK = MLA_NOPE + MLA_ROPE
DIL_WIDTH = MIX_WIDTH - MLA_WIDTH
DIL_HEAD_DIM = 128
DIL_HEADS = DIL_WIDTH // DIL_HEAD_DIM
DIL_PATTERNS = ((128, 1), (512, 4), (2048, 16))

IN_SPLITS = (MLA_Q_RANK, MLA_KV_RANK, MLA_ROPE, MLA_WIDTH, 3 * DIL_WIDTH, DIL_WIDTH)
IN_COLS = MLA_Q_RANK + MLA_KV_RANK + MLA_ROPE + MLA_WIDTH + 3 * DIL_WIDTH + DIL_WIDTH

kernel_name = "hybrid_mla_dilated_parallel_heads"


def rmsnorm(x, g):
    xf = x.astype(jnp.float32)
    y = xf * lax.rsqrt(jnp.mean(xf * xf, axis=-1, keepdims=True) + EPS)
    return (y * g.astype(jnp.float32)).astype(x.dtype)


def rope(x, pos):
    d = x.shape[-1]
    inv = ROPE_THETA ** (-jnp.arange(0, d, 2, dtype=jnp.float32) / d)
    ang = pos.astype(jnp.float32)[:, None] * inv[None, :]
    cos = jnp.cos(ang)[:, None, :]
    sin = jnp.sin(ang)[:, None, :]
    xf = x.astype(jnp.float32)
    x1, x2 = xf[..., : d // 2], xf[..., d // 2:]
    out = jnp.concatenate([x1 * cos - x2 * sin, x2 * cos + x1 * sin], axis=-1)
    return out.astype(x.dtype)


def split_cols(proj):
    parts, off = [], 0
    for n in IN_SPLITS:
        parts.append(proj[..., off:off + n])
        off += n
    return parts


def causal_block_attention(q, k, v, scale):
    B, S, H, Dk = q.shape
    Dv = v.shape[-1]
    nb = S // Q_BLOCK
    qb = q.astype(jnp.float32).reshape(B, nb, Q_BLOCK, H, Dk).transpose(1, 0, 2, 3, 4)
    kf = k.astype(jnp.float32)
    vf = v.astype(jnp.float32)
    kpos = jnp.arange(S)

    def one_block(args):
        i, qi = args
        s = jnp.einsum('bqhd,bkhd->bhqk', qi, kf) * scale
        qpos = i * Q_BLOCK + jnp.arange(Q_BLOCK)
        s = jnp.where(kpos[None, :] <= qpos[:, None], s, NEG)
        p = jax.nn.softmax(s, axis=-1)
        return jnp.einsum('bhqk,bkhd->bqhd', p, vf)

    out = lax.map(one_block, (jnp.arange(nb), qb))
    return out.transpose(1, 0, 2, 3, 4).reshape(B, S, H, Dv)


def dilated_window_attention(q, k, v, window, dilation):
    B, S, H, D = q.shape
    n_back = window // dilation
    L = S // dilation
    nb = -(-L // Q_BLOCK)
    Lp = nb * Q_BLOCK

    def sub(t):
        t = t.astype(jnp.float32).reshape(B, L, dilation, H, D).transpose(0, 2, 1, 3, 4)
        return jnp.pad(t, ((0, 0), (0, 0), (0, Lp - L), (0, 0), (0, 0)))

    def band(t):
        tp = jnp.pad(t, ((0, 0), (0, 0), (Q_BLOCK, 0), (0, 0), (0, 0)))
        prev = tp[:, :, :Lp].reshape(B, dilation, nb, Q_BLOCK, H, D)
        cur = t.reshape(B, dilation, nb, Q_BLOCK, H, D)
        return jnp.concatenate([prev, cur], axis=3)

    qb = sub(q).reshape(B, dilation, nb, Q_BLOCK, H, D)
    kb = band(sub(k))
    vb = band(sub(v))
    s = jnp.einsum('brnqhd,brnkhd->brnhqk', qb, kb) * (1.0 / math.sqrt(D))
    qi = jnp.arange(Q_BLOCK)[:, None]
    kj = jnp.arange(2 * Q_BLOCK)[None, :]
    dist = qi + Q_BLOCK - kj
    key_idx = jnp.arange(nb)[:, None, None] * Q_BLOCK + kj[None] - Q_BLOCK
    mask = (dist[None] >= 0) & (dist[None] <= n_back) & (key_idx >= 0)
    s = jnp.where(mask[:, None], s, NEG)
    m = jnp.max(s, axis=-1, keepdims=True)
    e = jnp.exp(s - m)
    den = jnp.sum(e, axis=-1, keepdims=True)
    o = jnp.einsum('brnhqk,brnkhd->brnqhd', e / den, vb)
    lse = (m + jnp.log(den))[..., 0]
    o = o.reshape(B, dilation, Lp, H, D)[:, :, :L].transpose(0, 2, 1, 3, 4).reshape(B, S, H, D)
    lse = lse.transpose(0, 1, 2, 4, 3).reshape(B, dilation, Lp, H)[:, :, :L]
    lse = lse.transpose(0, 2, 1, 3).reshape(B, S, H)
    return o, lse


def setup_inputs(seed: int = 0) -> dict:
    key = jax.random.key(seed)
    ks = jax.random.split(key, 16)

    def gain(k, n):
        return 1.0 + 0.02 * jax.random.normal(k, (DEPTH, n), jnp.float32)

    def w(k, fan_in, fan_out):
        return jax.random.normal(k, (DEPTH, fan_in, fan_out), jnp.float32) * fan_in ** -0.5

    return {
        "x": jax.random.normal(ks[0], (BATCH, SEQ, D_MODEL), jnp.float32),
        "norm_gain": gain(ks[1], D_MODEL),
        "w_in": w(ks[2], D_MODEL, IN_COLS),
        "q_a_norm_gain": gain(ks[3], MLA_Q_RANK),
        "kv_a_norm_gain": gain(ks[4], MLA_KV_RANK),
        "w_uq": w(ks[5], MLA_Q_RANK, MLA_HEADS * MLA_QK),
        "w_ukv": w(ks[6], MLA_KV_RANK, MLA_HEADS * (MLA_NOPE + MLA_V)),
        "mla_q_norm_gain": gain(ks[7], MLA_QK),
        "mla_k_norm_gain": gain(ks[8], MLA_QK),
        "dil_q_norm_gain": gain(ks[9], DIL_HEAD_DIM),
        "dil_k_norm_gain": gain(ks[10], DIL_HEAD_DIM),
        "mla_out_norm_gain": gain(ks[11], MLA_WIDTH),
        "dil_out_norm_gain": gain(ks[12], DIL_WIDTH),
        "w_out": w(ks[13], MIX_WIDTH, D_MODEL),
    }


def reference(x, norm_gain, w_in, q_a_norm_gain, kv_a_norm_gain, w_uq, w_ukv,
              mla_q_norm_gain, mla_k_norm_gain, dil_q_norm_gain, dil_k_norm_gain,
              mla_out_norm_gain, dil_out_norm_gain, w_out):
    B, S, _ = x.shape
    pos = jnp.arange(S, dtype=jnp.int32)
    h = x
    for l in range(DEPTH):
        hn = rmsnorm(h, norm_gain[l])
        proj = jnp.einsum('bsd,de->bse', hn, w_in[l])
        c_q, c_kv, k_r, g_a, qkv_b, g_b = split_cols(proj)

        gq, gk = mla_q_norm_gain[l], mla_k_norm_gain[l]
        q_a = jnp.einsum('bsr,re->bse', rmsnorm(c_q, q_a_norm_gain[l]), w_uq[l])
        q_a = q_a.reshape(B, S, MLA_HEADS, MLA_QK)
        kv_a = jnp.einsum('bsr,re->bse', rmsnorm(c_kv, kv_a_norm_gain[l]), w_ukv[l])
        kv_a = kv_a.reshape(B, S, MLA_HEADS, MLA_NOPE + MLA_V)
        q_nope = rmsnorm(q_a[..., :MLA_NOPE], gq[:MLA_NOPE])
        q_rope = rope(rmsnorm(q_a[..., MLA_NOPE:], gq[MLA_NOPE:]), pos)
        k_nope = rmsnorm(kv_a[..., :MLA_NOPE], gk[:MLA_NOPE])
        v_a = kv_a[..., MLA_NOPE:]
        k_rope = rope(rmsnorm(k_r, gk[MLA_NOPE:])[:, :, None, :], pos)
        q_full = jnp.concatenate([q_nope, q_rope], axis=-1)
        k_full = jnp.concatenate(
            [k_nope, jnp.broadcast_to(k_rope, (B, S, MLA_HEADS, MLA_ROPE))], axis=-1)
        o_a = causal_block_attention(q_full, k_full, v_a, 1.0 / math.sqrt(MLA_QK))
        o_a = o_a.reshape(B, S, MLA_WIDTH).astype(h.dtype)
        y_a = rmsnorm(o_a, mla_out_norm_gain[l]) * jax.nn.silu(g_a)

        q_b, k_b, v_b = jnp.split(qkv_b, 3, axis=-1)
        q_b = rope(rmsnorm(q_b.reshape(B, S, DIL_HEADS, DIL_HEAD_DIM), dil_q_norm_gain[l]), pos)
        k_b = rope(rmsnorm(k_b.reshape(B, S, DIL_HEADS, DIL_HEAD_DIM), dil_k_norm_gain[l]), pos)
        v_b = v_b.reshape(B, S, DIL_HEADS, DIL_HEAD_DIM)
        outs, lses = [], []
        for window, dilation in DIL_PATTERNS:
            o_p, lse_p = dilated_window_attention(q_b, k_b, v_b, window, dilation)
            outs.append(o_p)
            lses.append(lse_p)
        wts = jax.nn.softmax(jnp.stack(lses, axis=0), axis=0)
        o_b = jnp.sum(wts[..., None] * jnp.stack(outs, axis=0), axis=0)
        o_b = o_b.reshape(B, S, DIL_WIDTH).astype(h.dtype)
        y_b = rmsnorm(o_b, dil_out_norm_gain[l]) * jax.nn.silu(g_b)

        y = jnp.concatenate([y_a, y_b], axis=-1)
        h = h + jnp.einsum('bse,ed->bsd', y, w_out[l])
    return h
```

```python
import contextlib
import numpy as np
import ml_dtypes
import concourse.bass as bass
import concourse.mybir as mybir
from concourse.bass_utils import run_bass_kernel_spmd

F32 = mybir.dt.float32
BF16 = mybir.dt.bfloat16
AF = mybir.ActivationFunctionType
ALU = mybir.AluOpType
NPBF = ml_dtypes.bfloat16

D = 2048
T = 4096
TO = 2048
EPS = 1e-6
C_CQ, C_CKV, C_KR, C_GA, C_QB, C_KB, C_VB, C_GB = 0, 768, 1280, 1344, 2368, 3392, 4416, 5440
DEBUG = False
STOP = 9
NSETS = 99
JOBLIMIT = 10**9
_JOBCNT = [0]


class Prog:
    ENG = ("pe", "act", "dve", "pool", "sp")

    def __init__(self, nc):
        self.nc = nc
        self.streams = {e: [] for e in self.ENG}
        self.sem = {e: nc.alloc_semaphore(name=f"s_{e}") for e in ("pe", "act", "dve", "pool")}
        self.cnt = {e: 0 for e in self.sem}
        self.waited = {e: {} for e in self.ENG}
        self.dsems = {}
        self.dcnt = {}

    def _wait(self, eng, deps):
        for d in deps:
            if d is None:
                continue
            sem, val, key = d
            if self.waited[eng].get(key, 0) >= val:
                continue
            self.waited[eng][key] = val
            self.streams[eng].append(lambda e, sem=sem, val=val: e.wait_ge(sem, val))

    def op(self, eng, fn, deps=(), sig=True):
        self._wait(eng, deps)
        if sig:
            self.cnt[eng] += 1
            sem = self.sem[eng]
            self.streams[eng].append(lambda e, fn=fn, sem=sem: fn(e).then_inc(sem, 1))
            return (sem, self.cnt[eng], eng)
        self.streams[eng].append(lambda e, fn=fn: fn(e))
        return None

    def dma(self, q, dsem, fn, deps=()):
        self._wait(q, deps)
        if dsem not in self.dsems:
            self.dsems[dsem] = self.nc.alloc_semaphore(name=f"d_{dsem}")
            self.dcnt[dsem] = 0
        self.dcnt[dsem] += 16
        sem = self.dsems[dsem]
        self.streams[q].append(lambda e, fn=fn, sem=sem: fn(e).then_inc(sem, 16))
        return (sem, self.dcnt[dsem], "d_" + dsem)

    def all_tokens(self):
        toks = [(self.sem[e], self.cnt[e], e) for e in self.sem if self.cnt[e] > 0]
        toks += [(self.dsems[n], self.dcnt[n], "d_" + n) for n in self.dsems]
        return toks

    def barrier(self):
        toks = self.all_tokens()
        for e in self.ENG:
            self._wait(e, toks)

    def emit(self):
        nc = self.nc
        st = self.streams
        self.streams = {e: [] for e in self.ENG}
        with nc.Block() as block:
            @block.tensor
            def _(e):
                for f in st["pe"]:
                    f(e)

            @block.scalar
            def _(e):
                for f in st["act"]:
                    f(e)

            @block.vector
            def _(e):
                for f in st["dve"]:
                    f(e)

            @block.gpsimd
            def _(e):
                for f in st["pool"]:
                    f(e)

            @block.sync
            def _(e):
                for f in st["sp"]:
                    f(e)


def last_tokens(toks):
    best = {}
    for t in toks:
        if t is None:
            continue
        if t[2] not in best or best[t[2]][1] < t[1]:
            best[t[2]] = t
    return list(best.values())


class Slot:
    def __init__(self, t):
        self.t = t
        self.ready = None
        self.readers = []

    def wdeps(self):
        return list(self.readers) + ([self.ready] if self.ready else [])


def build_program():
    nc = bass.Bass("TRN2", target_bir_lowering=False)

    def din(name, shape, dt):
        return nc.dram_tensor(name, shape, dt, kind="ExternalInput").ap()

    def dscr(name, shape, dt):
        return nc.dram_tensor(name, shape, dt, kind="ExternalOutput" if DEBUG else "Internal").ap()

    x_perm = din("x_perm", [T, D], F32)
    w_in = din("w_in", [D, 6464], F32)
    w_uq = din("w_uq", [768, 1536], F32)
    w_ukv = din("w_ukv", [512, 2048], F32)
    w_out = din("w_out", [D, D], F32)
    gain_bc = din("gain_bc", [128, D], F32)
    gvec_d = din("gvec", [128, 32], F32)
    cs128 = din("cs128", [128, T], F32)
    sn128 = din("sn128", [128, T], F32)
    cs64 = din("cs64", [128, T], F32)
    sn64 = din("sn64", [128, T], F32)
    tabw_d = din("tabw", [2, 128, 1536], BF16)
    tabm_d = din("tabm", [2, 128, 128], BF16)
    consts_d = din("consts", [128, 6, 128], BF16)
    out_own = nc.dram_tensor("out_own", [TO, D], F32, kind="ExternalOutput").ap()

    CQT = dscr("CQT", [128, 6, TO], BF16)
    CKVT = dscr("CKVT", [128, 4, T], BF16)
    KR = dscr("KR", [64, T], BF16)
    SG = dscr("SG", [16, 128, TO], BF16)
    QTB = dscr("QTB", [8, 128, TO], BF16)
    KTB = dscr("KTB", [8, 128, T], BF16)
    VB = dscr("VB", [T, 1024], BF16)
    KTA = dscr("KTA", [8, 128, T], BF16)
    VA = dscr("VA", [T, 1024], BF16)
    QTA = dscr("QTA", [8, 128, TO], BF16)
    QRA = dscr("QRA", [4, 128, TO], BF16)

    P = Prog(nc)
    final_tokens = []
    uid = [0]

    def U(name):
        uid[0] += 1
        return f"t{uid[0]}_{name}"

    with contextlib.ExitStack() as esg:
        def SG_(name, shape, dt):
            return esg.enter_context(nc.sbuf_tensor(U(name), shape, dt))

        consts = SG_("consts", [128, 6, 128], BF16)
        gvec = SG_("gvec", [128, 32], F32)
        epsb = SG_("epsb", [128, 1], F32)
        ident = consts[:, 0, :]
        ones = consts[:, 1, :]
        R128 = consts[:, 2, :]
        R64 = consts[0:64, 3, 0:64]
        BD1 = consts[:, 4, :]
        BDR = consts[:, 5, :]
        t_c = P.dma("sp", "c0", lambda e: e.dma_start(out=consts[:], in_=consts_d))
        t_gv = P.dma("sp", "c1", lambda e: e.dma_start(out=gvec[:], in_=gvec_d))
        t_eps = P.op("dve", lambda e: e.memset(epsb[:], EPS))
        P.barrier()

        def make_post_env(es, S, PS, nacc=3, with_lat=True, nswp=2):
            env = {}
            env["acc"] = [Slot(PS(f"acc{i}", [128, 512], F32)) for i in range(nacc)]
            env["ssq"] = [Slot(PS(f"ssqp{i}", [128, 512], F32)) for i in range(2)]
            env["swp"] = [Slot(PS(f"swp{i}", [128, 512], F32)) for i in range(nswp)]
            env["lat"] = Slot(PS("latp", [128, 512], F32)) if with_lat else None
            env["mhalf"] = S("mhalf", [128, 512], F32)
            env["t_mh"] = P.op("pool", lambda e: e.memset(env["mhalf"][:], -0.5))
            env["sq"] = [Slot(S(f"sq{i}", [128, 512], BF16)) for i in range(2)]
            env["rt"] = [Slot(S(f"rt{i}", [128, 512], F32)) for i in range(2)]
            env["ah"] = [Slot(S(f"ah{i}", [128, 512], BF16)) for i in range(2)]
            env["r1"] = [Slot(S(f"r1{i}", [128, 512], F32)) for i in range(2)]
            env["r2"] = [Slot(S(f"r2{i}", [128, 512], F32)) for i in range(2)]
            env["ob"] = [Slot(S(f"ob{i}", [128, 512], BF16)) for i in range(6)]
            env["cs"] = [Slot(S(f"cs{i}", [128, 512], F32)) for i in range(2)]
            env["sn"] = [Slot(S(f"sn{i}", [128, 512], F32)) for i in range(2)]
            env["raw"] = [Slot(S(f"raw{i}", [128, 512], F32)) for i in range(6)]
            env["n"] = {k: 0 for k in ("acc", "ssq", "swp", "sq", "rt", "ah", "r1", "r2", "ob", "tab", "rs")}
            env["tabkey"] = [None, None]
            return env

        def nxt(env, k):
            lst = env[k]
            s = lst[env["n"][k] % len(lst)]
            env["n"][k] += 1
            return s

        def get_tables(env, kind, n, rows=None):
            rows = rows or kind
            key = (kind, n, rows)
            for i in range(2):
                if env["tabkey"][i] == key:
                    return env["cs"][i], env["sn"][i]
            i = env["n"]["tab"] % 2
            env["n"]["tab"] += 1
            env["tabkey"][i] = key
            cs, sn = env["cs"][i], env["sn"][i]
            Pn = rows
            csd, snd = (cs128, sn128) if kind == 128 else (cs64, sn64)
            cs.ready = P.dma("sp", f"cs{i}", lambda e: e.dma_start(out=cs.t[0:Pn, :], in_=csd[0:Pn, n * 512:(n + 1) * 512]), deps=cs.wdeps())
            cs.readers = []
            sn.ready = P.dma("sp", f"sn{i}", lambda e: e.dma_start(out=sn.t[0:Pn, :], in_=snd[0:Pn, n * 512:(n + 1) * 512]), deps=sn.wdeps())
            sn.readers = []
            return cs, sn

        def store(env, ob, M, dst, tok):
            i = env["ob"].index(ob)
            t = P.dma("sp", f"ob{i}", lambda e: e.dma_start(out=dst, in_=ob.t[0:M, :]), deps=[tok])
            ob.readers = [t]
            return t

        def main_fm(env, lhs_list, rhs_list, M, wdeps):
            acc = nxt(env, "acc")
            deps = list(wdeps) + acc.wdeps()
            nk = len(lhs_list)
            tok = None
            for k in range(nk):
                tok = P.op("pe", lambda e, k=k: e.matmul(acc.t[0:M, :], lhsT=lhs_list[k], rhs=rhs_list[k], start=(k == 0), stop=(k == nk - 1)),
                           deps=deps if k == 0 else (), sig=(k == nk - 1))
            acc.ready = tok
            acc.readers = []
            return acc, tok

        def rstd_tile(env, src_ps, M, d, dep, mode=0):
            rt = nxt(env, "rt")
            if mode == 0:
                t_rt = P.op("act", lambda e: e.activation(out=rt.t[0:M, :], in_=src_ps.t[0:M, :], func=AF.Ln, bias=epsb[0:M, :], scale=1.0 / d), deps=[dep, t_eps] + rt.wdeps())
                src_ps.readers.append(t_rt)
                t_rc = P.op("act", lambda e: e.activation(out=rt.t[0:M, :], in_=rt.t[0:M, :], func=AF.Exp, scale=-0.5), deps=[t_rt])
            else:
                t_rt = P.op("act", lambda e: e.activation(out=rt.t[0:M, :], in_=src_ps.t[0:M, :], func=AF.Sqrt, bias=epsb[0:M, :], scale=1.0 / d), deps=[dep, t_eps] + rt.wdeps())
                src_ps.readers.append(t_rt)
                t_rc = P.op("dve", lambda e: e.reciprocal(out=rt.t[0:M, :], in_=rt.t[0:M, :]), deps=[t_rt])
            rt.ready, rt.readers = t_rc, []
            return rt, t_rc

        def post_head(env, acc, M, gcol, d, rope, n, dst, ones_m=None, R_m=None, tkind=None):
            sq = nxt(env, "sq")
            t_sq = P.op("act", lambda e: e.activation(out=sq.t[0:M, :], in_=acc.t[0:M, :], func=AF.Square), deps=[acc.ready] + sq.wdeps())
            sq.ready, sq.readers = t_sq, []
            ssq = nxt(env, "ssq")
            om = ones_m if ones_m is not None else ones[0:M, 0:M]
            t_ss = P.op("pe", lambda e: e.matmul(ssq.t[0:M, :], lhsT=om, rhs=sq.t[0:M, :], start=True, stop=True), deps=[t_sq, t_c] + ssq.wdeps())
            ssq.ready, ssq.readers = t_ss, []
            sq.readers.append(t_ss)
            return lambda: post_head_b(env, acc, M, gcol, d, rope, n, dst, ssq, t_ss, t_sq, R_m, tkind)

        def post_head_b(env, acc, M, gcol, d, rope, n, dst, ssq, t_ss, t_sq, R_m=None, tkind=None):
            mode = 0
            if rope is None and env.get("alt_rstd"):
                mode = env["n"]["rs"] % 2
                env["n"]["rs"] += 1
            rt, t_rc = rstd_tile(env, ssq, M, d, t_ss, mode)
            if rope is None:
                ob = nxt(env, "ob")
                t_a = P.op("dve", lambda e: e.scalar_tensor_tensor(out=ob.t[0:M, :], in0=acc.t[0:M, :], scalar=gvec[0:M, gcol:gcol + 1], in1=rt.t[0:M, :], op0=ALU.mult, op1=ALU.mult),
                           deps=[t_rc, acc.ready, t_gv] + ob.wdeps())
                acc.readers += [t_sq, t_a]
                rt.readers.append(t_a)
                ob.ready = t_a
                store(env, ob, M, dst, t_a)
                return None
            Rm = R_m if R_m is not None else (R128 if M == 128 else R64)
            ah = nxt(env, "ah")
            t_a = P.op("dve", lambda e: e.scalar_tensor_tensor(out=ah.t[0:M, :], in0=acc.t[0:M, :], scalar=gvec[0:M, gcol:gcol + 1], in1=rt.t[0:M, :], op0=ALU.mult, op1=ALU.mult),
                       deps=[t_rc, acc.ready, t_gv] + ah.wdeps())
            acc.readers += [t_sq, t_a]
            rt.readers.append(t_a)
            ah.ready, ah.readers = t_a, []
            cs, sn = get_tables(env, tkind or M, n, rows=M)
            cs_ready, sn_ready = cs.ready, sn.ready

            def stage2():
                swp = nxt(env, "swp")
                t_sw = P.op("pe", lambda e: e.matmul(swp.t[0:M, :], lhsT=Rm, rhs=ah.t[0:M, :], start=True, stop=True), deps=[t_a, t_c] + swp.wdeps())
                swp.ready, swp.readers = t_sw, []
                r1 = nxt(env, "r1")
                t_r1 = P.op("pool", lambda e: e.tensor_tensor(out=r1.t[0:M, :], in0=ah.t[0:M, :], in1=cs.t[0:M, :], op=ALU.mult), deps=[t_a, cs_ready] + r1.wdeps())
                r1.ready, r1.readers = t_r1, []
                ah.readers.extend([t_sw, t_r1])
                r2 = nxt(env, "r2")
                t_r2 = P.op("dve", lambda e: e.tensor_tensor(out=r2.t[0:M, :], in0=swp.t[0:M, :], in1=sn.t[0:M, :], op=ALU.mult), deps=[t_sw, sn_ready] + r2.wdeps())
                r2.ready, r2.readers = t_r2, []
                swp.readers.append(t_r2)
                ob = nxt(env, "ob")
                t_o = P.op("pool", lambda e: e.tensor_tensor(out=ob.t[0:M, :], in0=r1.t[0:M, :], in1=r2.t[0:M, :], op=ALU.add), deps=[t_r1, t_r2] + ob.wdeps())
                r1.readers.append(t_o)
                r2.readers.append(t_o)
                ob.ready = t_o
                store(env, ob, M, dst, t_o)
                return t_r1, t_r2
            def cont():
                t_r1, t_r2 = stage2()
                cs.readers.append(t_r1)
                sn.readers.append(t_r2)
            return cont

        def post_silu(env, acc, dst):
            ob = nxt(env, "ob")
            t = P.op("act", lambda e: e.activation(out=ob.t[:], in_=acc.t[:], func=AF.Silu), deps=[acc.ready] + ob.wdeps())
            acc.readers.append(t)
            ob.ready = t
            store(env, ob, 128, dst, t)

        def post_copy(env, acc, dst, ncol, eng):
            ob = nxt(env, "ob")
            if eng == "act":
                t = P.op("act", lambda e: e.activation(out=ob.t[:, 0:ncol], in_=acc.t[:, 0:ncol], func=AF.Copy), deps=[acc.ready] + ob.wdeps())
            else:
                t = P.op("dve", lambda e: e.tensor_copy(out=ob.t[:, 0:ncol], in_=acc.t[:, 0:ncol]), deps=[acc.ready] + ob.wdeps())
            acc.readers.append(t)
            ob.ready = t
            i = env["ob"].index(ob)
            t2 = P.dma("sp", f"ob{i}", lambda e: e.dma_start(out=dst, in_=ob.t[:, 0:ncol]), deps=[t])
            ob.readers = [t2]

        def post_latent(env, acc, j, nj, gcol0, d, dsts):
            raw = env["raw"][j]
            sq = nxt(env, "sq")
            t_raw = P.op("dve", lambda e: e.tensor_copy(out=raw.t[:], in_=acc.t[:]), deps=[acc.ready] + raw.wdeps())
            raw.ready, raw.readers = t_raw, []
            t_sq = P.op("act", lambda e: e.activation(out=sq.t[:], in_=raw.t[:], func=AF.Square), deps=[t_raw] + sq.wdeps())
            sq.ready, sq.readers = t_sq, []
            raw.readers.append(t_sq)
            acc.readers += [t_raw]
            lat = env["lat"]
            t_ss = P.op("pe", lambda e: e.matmul(lat.t[:], lhsT=ones, rhs=sq.t[:], start=(j == 0), stop=(j == nj - 1)),
                        deps=[t_sq, t_c] + (lat.wdeps() if j == 0 else []))
            sq.readers.append(t_ss)
            if j < nj - 1:
                return
            lat.ready, lat.readers = t_ss, []
            rt, t_rc = rstd_tile(env, lat, 128, d, t_ss)
            for jj in range(nj):
                rw = env["raw"][jj]
                ob = nxt(env, "ob")
                t_a = P.op("dve", lambda e, rw=rw, ob=ob, jj=jj: e.scalar_tensor_tensor(out=ob.t[:], in0=rw.t[:], scalar=gvec[:, gcol0 + jj:gcol0 + jj + 1], in1=rt.t[:], op0=ALU.mult, op1=ALU.mult),
                           deps=[t_rc, rw.ready, t_gv] + ob.wdeps())
                rw.readers.append(t_a)
                rt.readers.append(t_a)
                ob.ready = t_a
                store(env, ob, 128, dsts[jj], t_a)

        def run_jobs(jobs, L1=1, L2=None):
            q1 = []
            stages = []

            def advance():
                nxt_stages = []
                if len(q1) > L1 or (drain[0] and q1):
                    p, r = q1.pop(0)
                    nxt_stages.append(p(r))
                else:
                    nxt_stages.append(None)
                for c in stages:
                    nxt_stages.append(c() if c is not None else None)
                while nxt_stages and nxt_stages[-1] is None:
                    nxt_stages.pop()
                stages[:] = nxt_stages
            drain = [False]
            for job in jobs:
                _JOBCNT[0] += 1
                if _JOBCNT[0] > JOBLIMIT:
                    break
                res = job[0]()
                q1.append((job[1], res))
                advance()
            drain[0] = True
            while q1 or stages:
                advance()

        with contextlib.ExitStack() as esA:
            def SA(name, shape, dt):
                return esA.enter_context(nc.sbuf_tensor(U(name), shape, dt))
            hnT = SA("hnT", [128, 16, T], BF16)

            with contextlib.ExitStack() as es1:
                def S1(name, shape, dt):
                    return es1.enter_context(nc.sbuf_tensor(U(name), shape, dt))

                def PS1(name, shape, dt):
                    return es1.enter_context(nc.psum_tensor(U(name), shape, dt))
                xr = [Slot(S1(f"xr{i}", [128, D], F32)) for i in range(3)]
                xn = [Slot(S1(f"xn{i}", [128, D], BF16)) for i in range(3)]
                junk = S1("junk", [128, D], BF16)
                gbc = S1("gbc", [128, D], F32)
                ssq1 = [S1(f"ssq1{i}", [128, 1], F32) for i in range(3)]
                rs1 = [Slot(S1(f"rs1{i}", [128, 1], F32)) for i in range(3)]
                tp = [[Slot(PS1(f"tp{i}{j}", [128, 8, 128], BF16)) for j in range(2)] for i in range(2)]
                t_g = P.dma("sp", "gbc", lambda e: e.dma_start(out=gbc[:], in_=gain_bc))
                CUT = 1280
                st1 = {}

                def stageA(tt):
                    s = tt % 3
                    X = xr[s]
                    t_x = P.dma("sp", f"x{s}", lambda e: e.dma_start(out=X.t[:], in_=x_perm[tt * 128:(tt + 1) * 128, :]), deps=X.wdeps())
                    X.ready, X.readers = t_x, []
                    t_sq = P.op("act", lambda e: e.activation(out=junk[:], in_=X.t[:], func=AF.Square, accum_out=ssq1[s][:]), deps=[t_x] + rs1[s].wdeps())
                    t_sr = P.op("act", lambda e: e.activation(out=rs1[s].t[:], in_=ssq1[s][:], func=AF.Sqrt, bias=epsb[:], scale=1.0 / D), deps=[t_sq, t_eps])
                    t_rc = P.op("dve", lambda e: e.reciprocal(out=rs1[s].t[:], in_=rs1[s].t[:]), deps=[t_sr])
                    t_xg = P.op("pool", lambda e: e.tensor_tensor(out=X.t[:, CUT:D], in0=X.t[:, CUT:D], in1=gbc[:, CUT:D], op=ALU.mult), deps=[t_sq, t_g, t_x])
                    rs1[s].ready, rs1[s].readers = t_rc, []
                    X.readers = [t_sq, t_xg]
                    st1[tt] = dict(t_x=t_x, t_rc=t_rc, t_xg=t_xg)

                def stageB(tt):
                    s = tt % 3
                    X, XN = xr[s], xn[s]
                    d = st1[tt]
                    t_xn = P.op("dve", lambda e: e.scalar_tensor_tensor(out=XN.t[:, 0:CUT], in0=X.t[:, 0:CUT], scalar=rs1[s].t[:, 0:1], in1=gbc[:, 0:CUT], op0=ALU.mult, op1=ALU.mult),
                                deps=[d["t_rc"], t_g, d["t_x"]] + XN.wdeps())
                    t_xn2 = P.op("act", lambda e: e.activation(out=XN.t[:, CUT:D], in_=X.t[:, CUT:D], func=AF.Copy, scale=rs1[s].t[:, 0:1]),
                                 deps=[d["t_xg"], d["t_rc"]] + XN.wdeps())
                    rs1[s].readers += [t_xn, t_xn2]
                    X.readers += [t_xn, t_xn2]
                    XN.ready, XN.readers = t_xn, []
                    toks = []
                    for j in range(2):
                        TP = tp[tt % 2][j]
                        tok = None
                        for cc in range(8):
                            ch = 8 * j + cc
                            tok = P.op("pe", lambda e, TP=TP, cc=cc, ch=ch: e.transpose(out=TP.t[:, cc, :], in_=XN.t[:, ch * 128:(ch + 1) * 128], identity=ident),
                                       deps=([t_xn, t_xn2, t_c] + TP.wdeps()) if cc == 0 else (), sig=(cc == 7))
                        TP.ready, TP.readers = tok, []
                        XN.readers.append(tok)
                        toks.append(tok)
                    d["toks"] = toks

                def stageC(tt):
                    d = st1[tt]
                    for j in range(2):
                        TP = tp[tt % 2][j]
                        tok = d["toks"][j]
                        if j == 0:
                            t_ev = P.op("act", lambda e, TP=TP: e.activation(out=hnT[:, 0:8, tt * 128:(tt + 1) * 128], in_=TP.t[:, :, :], func=AF.Copy), deps=[tok])
                        else:
                            t_ev = P.op("dve", lambda e, TP=TP: e.tensor_copy(out=hnT[:, 8:16, tt * 128:(tt + 1) * 128], in_=TP.t[:, :, :]), deps=[tok])
                        TP.readers.append(t_ev)

                for it in range(32 + 2):
                    if it < 32:
                        stageA(it)
                    if 0 <= it - 1 < 32:
                        stageB(it - 1)
                    if 0 <= it - 2 < 32:
                        stageC(it - 2)
                if DEBUG:
                    HNT = nc.dram_tensor("HNT", [128, 16, T], BF16, kind="ExternalOutput").ap()
                    P.barrier()
                    for c4 in range(4):
                        P.dma("sp", "dbg", lambda e, c4=c4: e.dma_start(out=HNT[:, 4 * c4:4 * c4 + 4, :], in_=hnT[:, 4 * c4:4 * c4 + 4, :]))
                P.barrier()
                P.emit()

            with contextlib.ExitStack() as es2:
              if STOP >= 2:
                def S2(name, shape, dt):
                    return es2.enter_context(nc.sbuf_tensor(U(name), shape, dt))

                def PS2(name, shape, dt):
                    return es2.enter_context(nc.psum_tensor(U(name), shape, dt))
                env = make_post_env(es2, S2, PS2)
                wr = [Slot(S2(f"wr{i}", [128, 16, 256], BF16)) for i in range(3)]
                wn = [0]

                def load_block(col0, ncol):
                    W = wr[wn[0] % 3]
                    i = wn[0] % 3
                    wn[0] += 1
                    tk = P.dma("pool", f"w{i}", lambda e: e.dma_start(out=W.t[:, :, 0:ncol], in_=w_in[:, col0:col0 + ncol].rearrange("(c p) n -> p c n", p=128)), deps=W.wdeps())
                    W.ready, W.readers = tk, []
                    return W

                def fm_job(W, coff, M, n, post):
                    def main():
                        lhs = [W.t[:, c, coff:coff + M] for c in range(16)]
                        rhs = [hnT[:, c, n * 512:(n + 1) * 512] for c in range(16)]
                        acc, tok = main_fm(env, lhs, rhs, M, [W.ready])
                        W.readers.append(tok)
                        return acc
                    return (main, post)

                def v_job(W, tile, half, ncol):
                    def main():
                        acc = nxt(env, "acc")
                        tok = None
                        for c in range(16):
                            tok = P.op("pe", lambda e, c=c, acc=acc: e.matmul(acc.t[:, 0:ncol], lhsT=hnT[:, c, tile * 128:(tile + 1) * 128], rhs=W.t[:, c, 0:ncol], start=(c == 0), stop=(c == 15)),
                                       deps=([W.ready] + acc.wdeps()) if c == 0 else (), sig=(c == 15))
                        acc.ready, acc.readers = tok, []
                        W.readers.append(tok)
                        return acc
                    return main

                sets = []
                def set_cq():
                    Ws = [load_block(C_CQ + 256 * b, 256) for b in range(3)]
                    jobs = []
                    for n in range(4):
                        dsts = [CQT[:, jj, n * 512:(n + 1) * 512] for jj in range(6)]
                        for j in range(6):
                            jobs.append(fm_job(Ws[j // 2], (j % 2) * 128, 128, n, lambda acc, j=j, dsts=dsts: post_latent(env, acc, j, 6, 0, 768, dsts)))
                    return jobs
                def set_ckv():
                    Ws = [load_block(C_CKV + 256 * b, 256) for b in range(2)]
                    Wk = load_block(C_KR, 64)
                    jobs = []
                    for n in range(8):
                        dsts = [CKVT[:, jj, n * 512:(n + 1) * 512] for jj in range(4)]
                        for j in range(4):
                            jobs.append(fm_job(Ws[j // 2], (j % 2) * 128, 128, n, lambda acc, j=j, dsts=dsts: post_latent(env, acc, j, 4, 6, 512, dsts)))
                        jobs.append(fm_job(Wk, 0, 64, n, lambda acc, n=n: post_head(env, acc, 64, 13, 64, True, n, KR[:, n * 512:(n + 1) * 512])))
                    return jobs

                def set_heads(col0, b, nchunks, gcol, dstT):
                    def f():
                        W = load_block(col0 + 256 * b, 256)
                        jobs = []
                        for n in range(nchunks):
                            for hh in range(2):
                                h = 2 * b + hh
                                jobs.append(fm_job(W, hh * 128, 128, n, lambda acc, n=n, h=h: post_head(env, acc, 128, gcol, 128, True, n, dstT[h, :, n * 512:(n + 1) * 512])))
                        return jobs
                    return f

                def set_gate(col0, b, hbase):
                    def f():
                        W = load_block(col0 + 256 * b, 256)
                        jobs = []
                        for n in range(4):
                            for hh in range(2):
                                h = hbase + 2 * b + hh
                                jobs.append(fm_job(W, hh * 128, 128, n, lambda acc, n=n, h=h: post_silu(env, acc, SG[h, :, n * 512:(n + 1) * 512])))
                        return jobs
                    return f

                def set_v(b):
                    def f():
                        W = load_block(C_VB + 256 * b, 256)
                        jobs = []
                        for tile in range(32):
                            jobs.append((v_job(W, tile, b, 256), lambda acc, tile=tile: post_copy(env, acc, VB[tile * 128:(tile + 1) * 128, b * 256:(b + 1) * 256], 256, "act" if tile % 2 == 0 else "dve")))
                        return jobs
                    return f

                sets.append(set_cq)
                sets.append(set_ckv)
                for b in range(4):
                    sets.append(set_heads(C_QB, b, 4, 14, QTB))
                for b in range(4):
                    sets.append(set_heads(C_KB, b, 8, 15, KTB))
                for b in range(4):
                    sets.append(set_v(b))
                for b in range(4):
                    sets.append(set_gate(C_GA, b, 0))
                for b in range(4):
                    sets.append(set_gate(C_GB, b, 8))
                alljobs = []
                for f in sets:
                    alljobs.append(f)
                alljobs = alljobs[:NSETS]
                built = alljobs[0]()
                for k in range(len(alljobs)):
                    cur = built
                    if k + 1 < len(alljobs) and k >= 2:
                        built = alljobs[k + 1]()
                        run_jobs(cur)
                    else:
                        run_jobs(cur)
                        if k + 1 < len(alljobs):
                            built = alljobs[k + 1]()
                P.barrier()
                P.emit()

        with contextlib.ExitStack() as es3:
          if STOP >= 3:
            def S3(name, shape, dt):
                return es3.enter_context(nc.sbuf_tensor(U(name), shape, dt))

            def PS3(name, shape, dt):
                return es3.enter_context(nc.psum_tensor(U(name), shape, dt))
            env = make_post_env(es3, S3, PS3, nacc=5, with_lat=False, nswp=1)
            ckvT = S3("ckvT", [128, 4, T], BF16)
            cqT = S3("cqT", [128, 6, TO], BF16)
            wk = S3("wk", [128, 4, 8, 128], BF16)
            wv = S3("wv", [128, 4, 1024], BF16)
            wqn = S3("wqn", [128, 6, 8, 128], BF16)
            wqr = S3("wqr", [128, 6, 512], BF16)
            ukv5 = w_ukv.rearrange("(kc p) (h two c) -> p kc h two c", p=128, two=2, c=128)
            uq_v = w_uq.rearrange("(kc p) (h c) -> p kc h c", p=128, c=192)
            tl = []
            for kc in range(4):
                tl.append(P.dma("pool", "wk", lambda e, kc=kc: e.dma_start(out=wk[:, kc, :, :], in_=ukv5[:, kc, :, 0, :])))
                tl.append(P.dma("pool", "wv", lambda e, kc=kc: e.dma_start(out=wv[:, kc, :].rearrange("p (h c) -> p h c", c=128), in_=ukv5[:, kc, :, 1, :])))
            for kc in range(6):
                tl.append(P.dma("pool", "wqn", lambda e, kc=kc: e.dma_start(out=wqn[:, kc, :, :], in_=uq_v[:, kc, :, 0:128])))
                tl.append(P.dma("pool", "wqr", lambda e, kc=kc: e.dma_start(out=wqr[:, kc, :].rearrange("p (h c) -> p h c", c=64), in_=uq_v[:, kc, :, 128:192])))
            for kc in range(4):
                tl.append(P.dma("sp", "ckv", lambda e, kc=kc: e.dma_start(out=ckvT[:, kc, :], in_=CKVT[:, kc, :])))
            for kc in range(6):
                tl.append(P.dma("sp", "cq", lambda e, kc=kc: e.dma_start(out=cqT[:, kc, :], in_=CQT[:, kc, :])))
            tl = last_tokens(tl)
            jobs = []

            def job_b(lhs_fn, rhs_fn, nk, M, post):
                def main():
                    acc, tok = main_fm(env, [lhs_fn(k) for k in range(nk)], [rhs_fn(k) for k in range(nk)], M, tl)
                    return acc
                return (main, post)
            jk, jqn, jqr, jv = [], [], [], []
            for h in range(8):
                for n in range(8):
                    jk.append(job_b(lambda k, h=h: wk[:, k, h, :], lambda k, n=n: ckvT[:, k, n * 512:(n + 1) * 512], 4, 128,
                                    lambda acc, h=h, n=n: post_head(env, acc, 128, 12, 128, None, n, KTA[h, :, n * 512:(n + 1) * 512])))
            for h in range(8):
                for n in range(4):
                    jqn.append(job_b(lambda k, h=h: wqn[:, k, h, :], lambda k, n=n: cqT[:, k, n * 512:(n + 1) * 512], 6, 128,
                                     lambda acc, h=h, n=n: post_head(env, acc, 128, 10, 128, None, n, QTA[h, :, n * 512:(n + 1) * 512])))
            for pr in range(4):
                for n in range(4):
                    jqr.append(job_b(lambda k, pr=pr: wqr[:, k, pr * 128:(pr + 1) * 128], lambda k, n=n: cqT[:, k, n * 512:(n + 1) * 512], 6, 128,
                                     lambda acc, pr=pr, n=n: post_head(env, acc, 128, 11, 64, True, n, QRA[pr, :, n * 512:(n + 1) * 512], ones_m=BD1, R_m=BDR, tkind=64)))
            for half in range(2):
                for tile in range(32):
                    def main(half=half, tile=tile):
                        acc = nxt(env, "acc")
                        tok = None
                        for k in range(4):
                            tok = P.op("pe", lambda e, k=k, acc=acc: e.matmul(acc.t[:, :], lhsT=ckvT[:, k, tile * 128:(tile + 1) * 128], rhs=wv[:, k, half * 512:(half + 1) * 512], start=(k == 0), stop=(k == 3)),
                                       deps=(tl + acc.wdeps()) if k == 0 else (), sig=(k == 3))
                        acc.ready, acc.readers = tok, []
                        return acc
                    jv.append((main, lambda acc, half=half, tile=tile: post_copy(env, acc, VA[tile * 128:(tile + 1) * 128, half * 512:(half + 1) * 512], 512, "dve")))
            for i in range(16):
                jobs += [jk[4 * i], jv[4 * i], jqn[2 * i], jk[4 * i + 1], jv[4 * i + 1], jqr[i],
                         jk[4 * i + 2], jv[4 * i + 2], jqn[2 * i + 1], jk[4 * i + 3], jv[4 * i + 3]]
            env["alt_rstd"] = False
            run_jobs(jobs, L1=2)
            P.barrier()
            P.emit()

        with contextlib.ExitStack() as esY:
          if STOP >= 4:
            def SY(name, shape, dt):
                return esY.enter_context(nc.sbuf_tensor(U(name), shape, dt))
            Y = SY("Y", [128, 16, TO], BF16)
            ssqacc = SY("ssqacc", [128, 2, 16], F32)
            t_z = P.op("dve", lambda e: e.memset(ssqacc[:], 0.0))
            wo = [Slot(SY(f"wo{i}", [128, 16, 512], BF16)) for i in range(3)]

            def load_wo(cb):
                W = wo[cb % 3]
                i = cb % 3
                toks = []
                dp = W.wdeps()
                for half in range(2):
                    toks.append(P.dma("pool", f"wo{i}", lambda e, half=half: e.dma_start(out=W.t[:, 8 * half:8 * half + 8, :], in_=w_out[half * 1024:(half + 1) * 1024, cb * 512:(cb + 1) * 512].rearrange("(c p) n -> p c n", p=128)), deps=dp))
                W.ready, W.readers = toks[-1], []
                return W
            Wlist = [load_wo(cb) for cb in range(3)]

            with contextlib.ExitStack() as es4:
                def S4(name, shape, dt):
                    return es4.enter_context(nc.sbuf_tensor(U(name), shape, dt))

                def PS4(name, shape, dt):
                    return es4.enter_context(nc.psum_tensor(U(name), shape, dt))
                KT = [Slot(S4(f"KT{i}", [128, T], BF16)) for i in range(2)]
                VV = [Slot(S4(f"VV{i}", [128, 32, 128], BF16)) for i in range(2)]
                QT = [Slot(S4(f"QT{i}", [128, TO], BF16)) for i in range(2)]
                QR = [Slot(S4(f"QR{i}", [128, TO], BF16)) for i in range(2)]
                SGs = [Slot(S4(f"SGs{i}", [128, TO], BF16)) for i in range(2)]
                KRs = S4("KRs", [128, T], BF16)
                tabw = S4("tabw", [128, 2, 1536], BF16)
                tabm = S4("tabm", [128, 2, 128], BF16)
                pt = [Slot(S4(f"pt{i}", [128, 512], BF16)) for i in range(6)]
                rden = S4("rden", [128, 512], F32)
                osb = S4("osb", [128, 512], F32)
                osq = S4("osq", [128, 512], BF16)
                sps = [Slot(PS4(f"sps{i}", [128, 512], F32)) for i in range(4)]
                ops_ = [Slot(PS4(f"ops{i}", [128, 512], F32)) for i in range(2)]
                dps = [Slot(PS4(f"dps{i}", [128, 512], F32)) for i in range(1)]
                ssp = Slot(PS4("ssp", [128, 4], F32))
                t_kr0 = P.dma("sp", "krs", lambda e: e.dma_start(out=KRs[0:64, :], in_=KR))
                t_kz = P.op("pool", lambda e: e.memset(KRs[64:128, :], 0.0))
                for qs in QR:
                    qs.ready = P.op("pool", lambda e, qs=qs: e.memset(qs.t[64:128, :], 0.0))
                t_qz = qs.ready
                t_kr = t_kr0
                t_tw = [P.dma("sp", "tw", lambda e, oo=oo: e.dma_start(out=tabw[:, oo, :], in_=tabw_d[oo])) for oo in range(2)]
                t_tm = [P.dma("sp", "tm", lambda e, oo=oo: e.dma_start(out=tabm[:, oo, :], in_=tabm_d[oo])) for oo in range(2)]
                tab_tok = last_tokens(t_tw + t_tm)

                def load_head(H):
                    s = H % 2
                    mla = H < 8
                    h = H % 8
                    ktd = KTA if mla else KTB
                    vd = VA if mla else VB
                    qd = QTA if mla else QTB
                    k_, v_, q_, g_ = KT[s], VV[s], QT[s], SGs[s]
                    k_.ready = P.dma("sp", f"hk{s}", lambda e: e.dma_start(out=k_.t[:], in_=ktd[h]), deps=k_.wdeps())
                    k_.readers = []
                    vtoks = []
                    vdeps = v_.wdeps()
                    for q4 in range(4):
                        vtoks.append(P.dma("sp", f"hv{s}", lambda e, q4=q4: e.dma_start(out=v_.t[:, 8 * q4:8 * q4 + 8, :], in_=vd[q4 * 1024:(q4 + 1) * 1024, h * 128:(h + 1) * 128].rearrange("(kt p) c -> p kt c", p=128)),
                                           deps=vdeps))
                    v_.ready, v_.readers = vtoks[-1], []
                    q_.ready = P.dma("sp", f"hq{s}", lambda e: e.dma_start(out=q_.t[:], in_=qd[h]), deps=q_.wdeps())
                    q_.readers = []
                    g_.ready = P.dma("sp", f"hg{s}", lambda e: e.dma_start(out=g_.t[:], in_=SG[H]), deps=g_.wdeps())
                    g_.readers = []
                    if mla:
                        r_ = QR[s]
                        r_.ready = P.dma("sp", f"hr{s}", lambda e: e.dma_start(out=r_.t[0:64, :], in_=QRA[h // 2, (h % 2) * 64:(h % 2) * 64 + 64, :]), deps=r_.wdeps())
                        r_.readers = []

                cnt = {"s": 0, "p": 0, "o": 0, "mk": 0}

                pend = {"fin": None, "tk": None, "ty": None}

                def do_chunk(H, qc, mla, k_, v_, q_, g_, r_, scale, gcol):
                    i_lo = 0 if mla else max(0, 4 * qc - 8)
                    tiles = []
                    for i in range(i_lo, 4 * qc + 4):
                        for oo in range(2):
                            r = i - 4 * qc
                            n0 = max(0, 2 * r) * 64
                            n1 = 512 if mla else min(8, 2 * r + 18) * 64
                            tiles.append((oo, i, r, n0, n1))
                    O = ops_[cnt["o"] % 2]
                    Dn = dps[0]
                    cnt["o"] += 1
                    nt = len(tiles)
                    sinfo = {}
                    q0 = qc * 512

                    def rec_qk(j):
                        oo, i, r, n0, n1 = tiles[j]
                        Sp = sps[cnt["s"] % 4]
                        cnt["s"] += 1
                        kcol = (0 if oo == 0 else TO) + i * 128
                        deps = [k_.ready, q_.ready] + Sp.wdeps()
                        if mla:
                            deps += [r_.ready, t_kr, t_kz, t_qz]
                            P.op("pe", lambda e: e.matmul(Sp.t[:, n0:n1], lhsT=k_.t[:, kcol:kcol + 128], rhs=q_.t[:, q0 + n0:q0 + n1], start=True, stop=False), deps=deps, sig=False)
                            tok = P.op("pe", lambda e: e.matmul(Sp.t[:, n0:n1], lhsT=KRs[:, kcol:kcol + 128], rhs=r_.t[:, q0 + n0:q0 + n1], start=False, stop=True))
                        else:
                            tok = P.op("pe", lambda e: e.matmul(Sp.t[:, n0:n1], lhsT=k_.t[:, kcol:kcol + 128], rhs=q_.t[:, q0 + n0:q0 + n1], start=True, stop=True), deps=deps)
                        Sp.ready, Sp.readers = tok, []
                        Pt = pt[cnt["p"] % 6]
                        cnt["p"] += 1
                        t_e = P.op("act", lambda e: e.activation(out=Pt.t[:, n0:n1], in_=Sp.t[:, n0:n1], func=AF.Exp, scale=float(scale)), deps=[tok] + Pt.wdeps())
                        Sp.readers.append(t_e)
                        Pt.ready, Pt.readers = t_e, []
                        if mla and r >= 0:
                            eng = "dve"
                            cnt["mk"] += 1
                            t_m = P.op(eng, lambda e: e.tensor_tensor(out=Pt.t[:, n0:n0 + 128], in0=Pt.t[:, n0:n0 + 128], in1=tabm[:, oo, :], op=ALU.mult), deps=[t_e] + tab_tok)
                            Pt.ready = t_m
                        elif not mla:
                            eng = "dve"
                            cnt["mk"] += 1
                            e0 = (n0 // 64 - 2 * r) * 64
                            t_m = P.op(eng, lambda e: e.tensor_tensor(out=Pt.t[:, n0:n1], in0=Pt.t[:, n0:n1], in1=tabw[:, oo, e0:e0 + (n1 - n0)], op=ALU.mult), deps=[t_e] + tab_tok)
                            Pt.ready = t_m
                        sinfo[j] = Pt

                    def rec_pv(j):
                        oo, i, r, n0, n1 = tiles[j]
                        Pt = sinfo[j]
                        kt = (0 if oo == 0 else 16) + i
                        deps = [Pt.ready, v_.ready, t_c] + ((O.wdeps() + Dn.wdeps()) if j == 0 else [])
                        P.op("pe", lambda e: e.matmul(O.t[:, n0:n1], lhsT=v_.t[:, kt, :], rhs=Pt.t[:, n0:n1], start=(j == 0), stop=(j == nt - 1), skip_group_check=True), deps=deps, sig=False)
                        tok = P.op("pe", lambda e: e.matmul(Dn.t[:, n0:n1], lhsT=ones, rhs=Pt.t[:, n0:n1], start=(j == 0), stop=(j == nt - 1), skip_group_check=True))
                        Pt.readers.append(tok)
                        return tok

                    for j0 in range(min(3, nt)):
                        rec_qk(j0)
                    last = None
                    for j in range(nt):
                        if j + 3 < nt:
                            rec_qk(j + 3)
                        last = rec_pv(j)
                        if j == 5 and pend["fin"] is not None:
                            pend["fin"]()
                            pend["fin"] = None
                    O.ready, Dn.ready = last, last
                    O.readers, Dn.readers = [], []
                    t_rd = P.op("dve", lambda e: e.reciprocal(out=rden[:], in_=Dn.t[:]), deps=[last])
                    t_o = P.op("dve", lambda e: e.tensor_tensor(out=osb[:], in0=O.t[:], in1=rden[:], op=ALU.mult), deps=[t_rd, last, pend["ty"]])
                    O.readers.append(t_o)
                    Dn.readers.append(t_rd)
                    t_q = P.op("act", lambda e: e.activation(out=osq[:], in_=osb[:], func=AF.Square), deps=[t_o, pend["tk"]])
                    k_.readers.append(last)
                    v_.readers.append(last)
                    q_.readers.append(last)
                    if mla:
                        r_.readers.append(last)

                    def fin_b():
                        tk = None
                        for tt in range(4):
                            tk = P.op("pe", lambda e, tt=tt: e.matmul(ssp.t[:, tt:tt + 1], lhsT=osq[:, tt * 128:(tt + 1) * 128], rhs=ones[:, 0:1], start=True, stop=True),
                                      deps=([t_q, t_c] + ssp.wdeps()) if tt == 0 else (), sig=(tt == 3))
                        mx = 0 if mla else 1
                        t_ac = P.op("dve", lambda e: e.tensor_tensor(out=ssqacc[:, mx, qc * 4:(qc + 1) * 4], in0=ssqacc[:, mx, qc * 4:(qc + 1) * 4], in1=ssp.t[:, 0:4], op=ALU.add), deps=[tk, t_z])
                        ssp.ready, ssp.readers = tk, [t_ac]
                        t_y = P.op("dve", lambda e: e.scalar_tensor_tensor(out=Y[:, H, q0:q0 + 512], in0=osb[:], scalar=gvec[:, gcol:gcol + 1], in1=g_.t[:, q0:q0 + 512], op0=ALU.mult, op1=ALU.mult),
                                   deps=[t_o, g_.ready, t_gv, t_q, tk])
                        pend["tk"], pend["ty"] = tk, t_y
                        g_.readers.append(t_y)
                    pend["fin"] = fin_b
                    pend["g"] = g_

                load_head(0)
                for H in range(16):
                    if pend["fin"] is not None:
                        pend["fin"]()
                        pend["fin"] = None
                    if H + 1 < 16:
                        load_head(H + 1)
                    s = H % 2
                    mla = H < 8
                    scale = 1.0 / np.sqrt(192.0) if mla else 1.0 / np.sqrt(128.0)
                    gcol = (16 if mla else 24) + (H % 8)
                    for qc in range(4):
                        do_chunk(H, qc, mla, KT[s], VV[s], QT[s], SGs[s], QR[s], scale, gcol)
                if pend["fin"] is not None:
                    pend["fin"]()
                P.barrier()
                P.emit()

            with contextlib.ExitStack() as es5:
              if STOP >= 5:
                def S5(name, shape, dt):
                    return es5.enter_context(nc.sbuf_tensor(U(name), shape, dt))

                def PS5(name, shape, dt):
                    return es5.enter_context(nc.psum_tensor(U(name), shape, dt))
                xo = [Slot(S5(f"xo{i}", [128, 512], F32)) for i in range(3)]
                t1 = [Slot(S5(f"t1{i}", [128, 512], F32)) for i in range(2)]
                ot = [Slot(S5(f"ot{i}", [128, 512], F32)) for i in range(3)]
                rsm = S5("rsm", [128, 2, 16], F32)
                pa = [Slot(PS5(f"pa{i}", [128, 512], F32)) for i in range(2)]
                pb = [Slot(PS5(f"pb{i}", [128, 512], F32)) for i in range(2)]
                t_r1 = P.op("act", lambda e: e.activation(out=rsm[:], in_=ssqacc[:], func=AF.Sqrt, bias=epsb[:], scale=1.0 / 1024.0), deps=[t_eps])
                t_r2 = P.op("dve", lambda e: e.reciprocal(out=rsm[:], in_=rsm[:]), deps=[t_r1])

                k = 0
                for cb in range(4):
                    Wc = Wlist[cb]
                    for tile in range(16):
                        X = xo[k % 3]
                        xi = k % 3
                        X.ready = P.dma("act", f"xo{xi}", lambda e, X=X, tile=tile, cb=cb: e.dma_start(out=X.t[:], in_=x_perm[tile * 128:(tile + 1) * 128, cb * 512:(cb + 1) * 512]), deps=X.wdeps())
                        X.readers = []
                        A, B = pa[k % 2], pb[k % 2]
                        tokA = tokB = None
                        for hh in range(8):
                            tokA = P.op("pe", lambda e, hh=hh, A=A, tile=tile, Wc=Wc: e.matmul(A.t[:], lhsT=Y[:, hh, tile * 128:(tile + 1) * 128], rhs=Wc.t[:, hh, :], start=(hh == 0), stop=(hh == 7)),
                                        deps=([Wc.ready] + A.wdeps()) if hh == 0 else (), sig=(hh == 7))
                        for hh in range(8):
                            tokB = P.op("pe", lambda e, hh=hh, B=B, tile=tile, Wc=Wc: e.matmul(B.t[:], lhsT=Y[:, 8 + hh, tile * 128:(tile + 1) * 128], rhs=Wc.t[:, 8 + hh, :], start=(hh == 0), stop=(hh == 7)),
                                        deps=B.wdeps() if hh == 0 else (), sig=(hh == 7))
                        Wc.readers.append(tokB)
                        A.ready, B.ready = tokA, tokB
                        T1 = t1[k % 2]
                        OT = ot[k % 3]
                        oi = k % 3
                        t_a = P.op("dve", lambda e, A=A, X=X, T1=T1, tile=tile: e.scalar_tensor_tensor(out=T1.t[:], in0=A.t[:], scalar=rsm[:, 0, tile:tile + 1], in1=X.t[:], op0=ALU.mult, op1=ALU.add),
                                   deps=[tokA, X.ready, t_r2] + T1.wdeps())
                        A.readers = [t_a]
                        X.readers = [t_a]
                        T1.ready, T1.readers = t_a, []
                        t_b = P.op("dve", lambda e, B=B, T1=T1, OT=OT, tile=tile: e.scalar_tensor_tensor(out=OT.t[:], in0=B.t[:], scalar=rsm[:, 1, tile:tile + 1], in1=T1.t[:], op0=ALU.mult, op1=ALU.add),
                                   deps=[tokB, t_a] + OT.wdeps())
                        B.readers = [t_b]
                        T1.readers = [t_b]
                        t_st = P.dma("sp", f"ot{oi}", lambda e, OT=OT, tile=tile, cb=cb: e.dma_start(out=out_own[tile * 128:(tile + 1) * 128, cb * 512:(cb + 1) * 512], in_=OT.t[:]), deps=[t_b])
                        OT.ready, OT.readers = None, [t_st]
                        final_tokens.append(t_st)
                        k += 1
                    if cb == 0:
                        Wlist.append(load_wo(3))
                P.barrier()
                P.emit()
    return nc


def _rope_tables(pos, d):
    half = d // 2
    inv = (np.float32(10000.0) ** (-np.arange(0, d, 2, dtype=np.float32) / np.float32(d))).astype(np.float32)
    ang = (pos.astype(np.float32)[None, :] * inv[:, None]).astype(np.float32)
    c = np.cos(ang).astype(np.float32)
    s = np.sin(ang).astype(np.float32)
    return np.concatenate([c, c], 0), np.concatenate([-s, s], 0)


def _mask_tables(c):
    kk = np.arange(64)[:, None]
    n = np.arange(64)[None, :]
    tabw = np.zeros((2, 128, 24 * 64), np.float32)
    tabm = np.zeros((2, 128, 128), np.float32)
    for oo in range(2):
        half_k = c if oo == 0 else 1 - c
        for s in range(2):
            for E in range(24):
                delta = E - s
                diff = 128 * delta + 64 * (c - half_k) + (n - kk)
                ge = diff >= 0
                w = (ge & (diff <= 128)).astype(np.float32) + (ge & (diff % 4 == 0) & (diff <= 512)) + (ge & (diff % 16 == 0) & (diff <= 2048))
                tabw[oo, s * 64:(s + 1) * 64, E * 64:(E + 1) * 64] = w
                if E < 2:
                    tabm[oo, s * 64:(s + 1) * 64, E * 64:(E + 1) * 64] = ge.astype(np.float32)
    return tabw.astype(NPBF), tabm.astype(NPBF)


def _perm(c):
    jb = np.arange(32)[:, None]
    i = np.arange(64)[None, :]
    own = (128 * jb + 64 * c + i).reshape(-1)
    oth = (128 * jb + 64 * (1 - c) + i).reshape(-1)
    return np.concatenate([own, oth])


def _consts():
    cst = np.zeros((128, 6, 128), np.float32)
    cst[0:64, 4, 0:64] = 1.0
    cst[64:128, 4, 64:128] = 1.0
    qq = np.arange(64)
    cst[qq, 5, (qq + 32) % 64] = 1.0
    cst[64 + qq, 5, 64 + (qq + 32) % 64] = 1.0
    cst[:, 0, :] = np.eye(128)
    cst[:, 1, :] = 1.0
    p = np.arange(128)
    cst[p, 2, (p + 64) % 128] = 1.0
    q = np.arange(64)
    cst[q, 3, (q + 32) % 64] = 1.0
    return cst.astype(NPBF)


def _gvec(qa, kva, gq, gk, dq, dk, mo, do):
    g = np.zeros((128, 32), np.float32)
    g[:, 0:6] = qa.reshape(6, 128).T
    g[:, 6:10] = kva.reshape(4, 128).T
    g[:, 10] = gq[:128]
    g[0:64, 11] = gq[128:]
    g[64:128, 11] = gq[128:]
    g[:, 12] = gk[:128]
    g[0:64, 13] = gk[128:]
    g[:, 14] = dq
    g[:, 15] = dk
    g[:, 16:24] = mo.reshape(8, 128).T
    g[:, 24:32] = do.reshape(8, 128).T
    return g


_NC_CACHE = {}


def make_in_maps(x, norm_gain, w_in, q_a_norm_gain, kv_a_norm_gain, w_uq, w_ukv,
                 mla_q_norm_gain, mla_k_norm_gain, dil_q_norm_gain, dil_k_norm_gain,
                 mla_out_norm_gain, dil_out_norm_gain, w_out):
    f = lambda a: np.ascontiguousarray(np.asarray(a, dtype=np.float32))
    x = f(x)
    w_in0, w_uq0, w_ukv0, w_out0 = f(w_in)[0], f(w_uq)[0], f(w_ukv)[0], f(w_out)[0]
    gain_bc = np.ascontiguousarray(np.broadcast_to(f(norm_gain)[0][None, :], (128, D)))
    gv = _gvec(f(q_a_norm_gain)[0], f(kv_a_norm_gain)[0], f(mla_q_norm_gain)[0], f(mla_k_norm_gain)[0],
               f(dil_q_norm_gain)[0], f(dil_k_norm_gain)[0], f(mla_out_norm_gain)[0], f(dil_out_norm_gain)[0])
    cst = _consts()
    in_maps = []
    perms = []
    for core in range(8):
        b, c = core // 2, core % 2
        pm = _perm(c)
        perms.append(pm)
        cs128, sn128 = _rope_tables(pm, 128)
        cs64, sn64 = _rope_tables(pm, 64)
        cs64 = np.ascontiguousarray(np.concatenate([cs64, cs64], 0))
        sn64 = np.ascontiguousarray(np.concatenate([sn64, sn64], 0))
        tabw, tabm = _mask_tables(c)
        in_maps.append({
            "x_perm": np.ascontiguousarray(x[b][pm]),
            "w_in": w_in0, "w_uq": w_uq0, "w_ukv": w_ukv0, "w_out": w_out0,
            "gain_bc": gain_bc, "gvec": gv,
            "cs128": cs128, "sn128": sn128, "cs64": cs64, "sn64": sn64,
            "tabw": tabw, "tabm": tabm, "consts": cst,
        })
    return in_maps, perms


def kernel(**inputs):
    in_maps, perms = make_in_maps(**inputs)
    if "nc" not in _NC_CACHE:
        _NC_CACHE["nc"] = build_program()
    nc = _NC_CACHE["nc"]
    res = run_bass_kernel_spmd(nc, in_maps, core_ids=list(range(8)))
    out = np.empty((4, T, D), np.float32)
    for core in range(8):
        b = core // 2
        out[b][perms[core][:TO]] = np.asarray(res.results[core]["out_own"], dtype=np.float32)
    return out
```

```python
import contextlib
import numpy as np
import ml_dtypes
import concourse.bass as bass
import concourse.mybir as mybir
from concourse.bass_utils import run_bass_kernel_spmd

F32 = mybir.dt.float32
BF16 = mybir.dt.bfloat16
AF = mybir.ActivationFunctionType
ALU = mybir.AluOpType
NPBF = ml_dtypes.bfloat16

D = 2048
T = 4096
TO = 2048
EPS = 1e-6
C_CQ, C_CKV, C_KR, C_GA, C_QB, C_KB, C_VB, C_GB = 0, 768, 1280, 1344, 2368, 3392, 4416, 5440
DEBUG = False
STOP = 9
NSETS = 99
JOBLIMIT = 10**9
_JOBCNT = [0]


class Prog:
    ENG = ("pe", "act", "dve", "pool", "sp")

    def __init__(self, nc):
        self.nc = nc
        self.streams = {e: [] for e in self.ENG}
        self.sem = {e: nc.alloc_semaphore(name=f"s_{e}") for e in ("pe", "act", "dve", "pool")}
        self.cnt = {e: 0 for e in self.sem}
        self.waited = {e: {} for e in self.ENG}
        self.dsems = {}
        self.dcnt = {}

    def _wait(self, eng, deps):
        for d in deps:
            if d is None:
                continue
            sem, val, key = d
            if self.waited[eng].get(key, 0) >= val:
                continue
            self.waited[eng][key] = val
            self.streams[eng].append(lambda e, sem=sem, val=val: e.wait_ge(sem, val))

    def op(self, eng, fn, deps=(), sig=True):
        self._wait(eng, deps)
        if sig:
            self.cnt[eng] += 1
            sem = self.sem[eng]
            self.streams[eng].append(lambda e, fn=fn, sem=sem: fn(e).then_inc(sem, 1))
            return (sem, self.cnt[eng], eng)
        self.streams[eng].append(lambda e, fn=fn: fn(e))
        return None

    def dma(self, q, dsem, fn, deps=()):
        self._wait(q, deps)
        if dsem not in self.dsems:
            self.dsems[dsem] = self.nc.alloc_semaphore(name=f"d_{dsem}")
            self.dcnt[dsem] = 0
        self.dcnt[dsem] += 16
        sem = self.dsems[dsem]
        self.streams[q].append(lambda e, fn=fn, sem=sem: fn(e).then_inc(sem, 16))
        return (sem, self.dcnt[dsem], "d_" + dsem)

    def all_tokens(self):
        toks = [(self.sem[e], self.cnt[e], e) for e in self.sem if self.cnt[e] > 0]
        toks += [(self.dsems[n], self.dcnt[n], "d_" + n) for n in self.dsems]
        return toks

    def barrier(self):
        toks = self.all_tokens()
        for e in self.ENG:
            self._wait(e, toks)

    def emit(self):
        nc = self.nc
        st = self.streams
        self.streams = {e: [] for e in self.ENG}
        with nc.Block() as block:
            @block.tensor
            def _(e):
                for f in st["pe"]:
                    f(e)

            @block.scalar
            def _(e):
                for f in st["act"]:
                    f(e)

            @block.vector
            def _(e):
                for f in st["dve"]:
                    f(e)

            @block.gpsimd
            def _(e):
                for f in st["pool"]:
                    f(e)

            @block.sync
            def _(e):
                for f in st["sp"]:
                    f(e)


def last_tokens(toks):
    best = {}
    for t in toks:
        if t is None:
            continue
        if t[2] not in best or best[t[2]][1] < t[1]:
            best[t[2]] = t
    return list(best.values())


class Slot:
    def __init__(self, t):
        self.t = t
        self.ready = None
        self.readers = []

    def wdeps(self):
        return list(self.readers) + ([self.ready] if self.ready else [])


def build_program():
    nc = bass.Bass("TRN2", target_bir_lowering=False)

    def din(name, shape, dt):
        return nc.dram_tensor(name, shape, dt, kind="ExternalInput").ap()

    def dscr(name, shape, dt):
        return nc.dram_tensor(name, shape, dt, kind="ExternalOutput" if DEBUG else "Internal").ap()

    x_perm = din("x_perm", [T, D], F32)
    w_in = din("w_in", [D, 6464], F32)
    w_uq = din("w_uq", [768, 1536], F32)
    w_ukv = din("w_ukv", [512, 2048], F32)
    w_out = din("w_out", [D, D], F32)
    gain_bc = din("gain_bc", [128, D], F32)
    gvec_d = din("gvec", [128, 32], F32)
    cs128 = din("cs128", [128, T], F32)
    sn128 = din("sn128", [128, T], F32)
    cs64 = din("cs64", [128, T], F32)
    sn64 = din("sn64", [128, T], F32)
    tabw_d = din("tabw", [2, 128, 1536], BF16)
    tabm_d = din("tabm", [2, 128, 128], BF16)
    consts_d = din("consts", [128, 6, 128], BF16)
    out_own = nc.dram_tensor("out_own", [TO, D], F32, kind="ExternalOutput").ap()

    CQT = dscr("CQT", [128, 6, TO], BF16)
    CKVT = dscr("CKVT", [128, 4, T], BF16)
    KR = dscr("KR", [64, T], BF16)
    SG = dscr("SG", [16, 128, TO], BF16)
    QTB = dscr("QTB", [8, 128, TO], BF16)
    KTB = dscr("KTB", [8, 128, T], BF16)
    VB = dscr("VB", [T, 1024], BF16)
    KTA = dscr("KTA", [8, 128, T], BF16)
    VA = dscr("VA", [T, 1024], BF16)
    QTA = dscr("QTA", [8, 128, TO], BF16)
    QRA = dscr("QRA", [4, 128, TO], BF16)

    P = Prog(nc)
    final_tokens = []
    uid = [0]

    def U(name):
        uid[0] += 1
        return f"t{uid[0]}_{name}"

    with contextlib.ExitStack() as esg:
        def SG_(name, shape, dt):
            return esg.enter_context(nc.sbuf_tensor(U(name), shape, dt))

        consts = SG_("consts", [128, 6, 128], BF16)
        gvec = SG_("gvec", [128, 32], F32)
        epsb = SG_("epsb", [128, 1], F32)
        ident = consts[:, 0, :]
        ones = consts[:, 1, :]
        R128 = consts[:, 2, :]
        R64 = consts[0:64, 3, 0:64]
        BD1 = consts[:, 4, :]
        BDR = consts[:, 5, :]
        t_c = P.dma("sp", "c0", lambda e: e.dma_start(out=consts[:], in_=consts_d))
        t_gv = P.dma("sp", "c1", lambda e: e.dma_start(out=gvec[:], in_=gvec_d))
        t_eps = P.op("dve", lambda e: e.memset(epsb[:], EPS))
        P.barrier()

        def make_post_env(es, S, PS, nacc=3, with_lat=True, nswp=2):
            env = {}
            env["acc"] = [Slot(PS(f"acc{i}", [128, 512], F32)) for i in range(nacc)]
            env["ssq"] = [Slot(PS(f"ssqp{i}", [128, 512], F32)) for i in range(2)]
            env["swp"] = [Slot(PS(f"swp{i}", [128, 512], F32)) for i in range(nswp)]
            env["lat"] = Slot(PS("latp", [128, 512], F32)) if with_lat else None
            env["mhalf"] = S("mhalf", [128, 512], F32)
            env["t_mh"] = P.op("pool", lambda e: e.memset(env["mhalf"][:], -0.5))
            env["sq"] = [Slot(S(f"sq{i}", [128, 512], BF16)) for i in range(2)]
            env["rt"] = [Slot(S(f"rt{i}", [128, 512], F32)) for i in range(2)]
            env["ah"] = [Slot(S(f"ah{i}", [128, 512], BF16)) for i in range(2)]
            env["r1"] = [Slot(S(f"r1{i}", [128, 512], F32)) for i in range(2)]
            env["r2"] = [Slot(S(f"r2{i}", [128, 512], F32)) for i in range(2)]
            env["ob"] = [Slot(S(f"ob{i}", [128, 512], BF16)) for i in range(6)]
            env["cs"] = [Slot(S(f"cs{i}", [128, 512], F32)) for i in range(2)]
            env["sn"] = [Slot(S(f"sn{i}", [128, 512], F32)) for i in range(2)]
            env["raw"] = [Slot(S(f"raw{i}", [128, 512], F32)) for i in range(6)]
            env["n"] = {k: 0 for k in ("acc", "ssq", "swp", "sq", "rt", "ah", "r1", "r2", "ob", "tab", "rs")}
            env["tabkey"] = [None, None]
            return env

        def nxt(env, k):
            lst = env[k]
            s = lst[env["n"][k] % len(lst)]
            env["n"][k] += 1
            return s

        def get_tables(env, kind, n, rows=None):
            rows = rows or kind
            key = (kind, n, rows)
            for i in range(2):
                if env["tabkey"][i] == key:
                    return env["cs"][i], env["sn"][i]
            i = env["n"]["tab"] % 2
            env["n"]["tab"] += 1
            env["tabkey"][i] = key
            cs, sn = env["cs"][i], env["sn"][i]
            Pn = rows
            csd, snd = (cs128, sn128) if kind == 128 else (cs64, sn64)
            cs.ready = P.dma("sp", f"cs{i}", lambda e: e.dma_start(out=cs.t[0:Pn, :], in_=csd[0:Pn, n * 512:(n + 1) * 512]), deps=cs.wdeps())
            cs.readers = []
            sn.ready = P.dma("sp", f"sn{i}", lambda e: e.dma_start(out=sn.t[0:Pn, :], in_=snd[0:Pn, n * 512:(n + 1) * 512]), deps=sn.wdeps())
            sn.readers = []
            return cs, sn

        def store(env, ob, M, dst, tok):
            i = env["ob"].index(ob)
            t = P.dma("sp", f"ob{i}", lambda e: e.dma_start(out=dst, in_=ob.t[0:M, :]), deps=[tok])
            ob.readers = [t]
            return t

        def main_fm(env, lhs_list, rhs_list, M, wdeps):
            acc = nxt(env, "acc")
            deps = list(wdeps) + acc.wdeps()
            nk = len(lhs_list)
            tok = None
            for k in range(nk):
                tok = P.op("pe", lambda e, k=k: e.matmul(acc.t[0:M, :], lhsT=lhs_list[k], rhs=rhs_list[k], start=(k == 0), stop=(k == nk - 1)),
                           deps=deps if k == 0 else (), sig=(k == nk - 1))
            acc.ready = tok
            acc.readers = []
            return acc, tok

        def rstd_tile(env, src_ps, M, d, dep, mode=0):
            rt = nxt(env, "rt")
            if mode == 0:
                t_rt = P.op("act", lambda e: e.activation(out=rt.t[0:M, :], in_=src_ps.t[0:M, :], func=AF.Ln, bias=epsb[0:M, :], scale=1.0 / d), deps=[dep, t_eps] + rt.wdeps())
                src_ps.readers.append(t_rt)
                t_rc = P.op("act", lambda e: e.activation(out=rt.t[0:M, :], in_=rt.t[0:M, :], func=AF.Exp, scale=-0.5), deps=[t_rt])
            else:
                t_rt = P.op("act", lambda e: e.activation(out=rt.t[0:M, :], in_=src_ps.t[0:M, :], func=AF.Sqrt, bias=epsb[0:M, :], scale=1.0 / d), deps=[dep, t_eps] + rt.wdeps())
                src_ps.readers.append(t_rt)
                t_rc = P.op("dve", lambda e: e.reciprocal(out=rt.t[0:M, :], in_=rt.t[0:M, :]), deps=[t_rt])
            rt.ready, rt.readers = t_rc, []
            return rt, t_rc

        def post_head(env, acc, M, gcol, d, rope, n, dst, ones_m=None, R_m=None, tkind=None):
            sq = nxt(env, "sq")
            t_sq = P.op("act", lambda e: e.activation(out=sq.t[0:M, :], in_=acc.t[0:M, :], func=AF.Square), deps=[acc.ready] + sq.wdeps())
            sq.ready, sq.readers = t_sq, []
            ssq = nxt(env, "ssq")
            om = ones_m if ones_m is not None else ones[0:M, 0:M]
            t_ss = P.op("pe", lambda e: e.matmul(ssq.t[0:M, :], lhsT=om, rhs=sq.t[0:M, :], start=True, stop=True), deps=[t_sq, t_c] + ssq.wdeps())
            ssq.ready, ssq.readers = t_ss, []
            sq.readers.append(t_ss)
            return lambda: post_head_b(env, acc, M, gcol, d, rope, n, dst, ssq, t_ss, t_sq, R_m, tkind)

        def post_head_b(env, acc, M, gcol, d, rope, n, dst, ssq, t_ss, t_sq, R_m=None, tkind=None):
            mode = 0
            if rope is None and env.get("alt_rstd"):
                mode = env["n"]["rs"] % 2
                env["n"]["rs"] += 1
            rt, t_rc = rstd_tile(env, ssq, M, d, t_ss, mode)
            if rope is None:
                ob = nxt(env, "ob")
                t_a = P.op("dve", lambda e: e.scalar_tensor_tensor(out=ob.t[0:M, :], in0=acc.t[0:M, :], scalar=gvec[0:M, gcol:gcol + 1], in1=rt.t[0:M, :], op0=ALU.mult, op1=ALU.mult),
                           deps=[t_rc, acc.ready, t_gv] + ob.wdeps())
                acc.readers += [t_sq, t_a]
                rt.readers.append(t_a)
                ob.ready = t_a
                store(env, ob, M, dst, t_a)
                return None
            Rm = R_m if R_m is not None else (R128 if M == 128 else R64)
            ah = nxt(env, "ah")
            t_a = P.op("dve", lambda e: e.scalar_tensor_tensor(out=ah.t[0:M, :], in0=acc.t[0:M, :], scalar=gvec[0:M, gcol:gcol + 1], in1=rt.t[0:M, :], op0=ALU.mult, op1=ALU.mult),
                       deps=[t_rc, acc.ready, t_gv] + ah.wdeps())
            acc.readers += [t_sq, t_a]
            rt.readers.append(t_a)
            ah.ready, ah.readers = t_a, []
            cs, sn = get_tables(env, tkind or M, n, rows=M)
            cs_ready, sn_ready = cs.ready, sn.ready

            def stage2():
                swp = nxt(env, "swp")
                t_sw = P.op("pe", lambda e: e.matmul(swp.t[0:M, :], lhsT=Rm, rhs=ah.t[0:M, :], start=True, stop=True), deps=[t_a, t_c] + swp.wdeps())
                swp.ready, swp.readers = t_sw, []
                r1 = nxt(env, "r1")
                t_r1 = P.op("pool", lambda e: e.tensor_tensor(out=r1.t[0:M, :], in0=ah.t[0:M, :], in1=cs.t[0:M, :], op=ALU.mult), deps=[t_a, cs_ready] + r1.wdeps())
                r1.ready, r1.readers = t_r1, []
                ah.readers.extend([t_sw, t_r1])
                r2 = nxt(env, "r2")
                t_r2 = P.op("dve", lambda e: e.tensor_tensor(out=r2.t[0:M, :], in0=swp.t[0:M, :], in1=sn.t[0:M, :], op=ALU.mult), deps=[t_sw, sn_ready] + r2.wdeps())
                r2.ready, r2.readers = t_r2, []
                swp.readers.append(t_r2)
                ob = nxt(env, "ob")
                t_o = P.op("pool", lambda e: e.tensor_tensor(out=ob.t[0:M, :], in0=r1.t[0:M, :], in1=r2.t[0:M, :], op=ALU.add), deps=[t_r1, t_r2] + ob.wdeps())
                r1.readers.append(t_o)
                r2.readers.append(t_o)
                ob.ready = t_o
                store(env, ob, M, dst, t_o)
                return t_r1, t_r2
            def cont():
                t_r1, t_r2 = stage2()
                cs.readers.append(t_r1)
                sn.readers.append(t_r2)
            return cont

        def post_silu(env, acc, dst):
            ob = nxt(env, "ob")
            t = P.op("act", lambda e: e.activation(out=ob.t[:], in_=acc.t[:], func=AF.Silu), deps=[acc.ready] + ob.wdeps())
            acc.readers.append(t)
            ob.ready = t
            store(env, ob, 128, dst, t)

        def post_copy(env, acc, dst, ncol, eng):
            ob = nxt(env, "ob")
            if eng == "act":
                t = P.op("act", lambda e: e.activation(out=ob.t[:, 0:ncol], in_=acc.t[:, 0:ncol], func=AF.Copy), deps=[acc.ready] + ob.wdeps())
            else:
                t = P.op("dve", lambda e: e.tensor_copy(out=ob.t[:, 0:ncol], in_=acc.t[:, 0:ncol]), deps=[acc.ready] + ob.wdeps())
            acc.readers.append(t)
            ob.ready = t
            i = env["ob"].index(ob)
            t2 = P.dma("sp", f"ob{i}", lambda e: e.dma_start(out=dst, in_=ob.t[:, 0:ncol]), deps=[t])
            ob.readers = [t2]

        def post_latent(env, acc, j, nj, gcol0, d, dsts):
            raw = env["raw"][j]
            sq = nxt(env, "sq")
            t_raw = P.op("dve", lambda e: e.tensor_copy(out=raw.t[:], in_=acc.t[:]), deps=[acc.ready] + raw.wdeps())
            raw.ready, raw.readers = t_raw, []
            t_sq = P.op("act", lambda e: e.activation(out=sq.t[:], in_=raw.t[:], func=AF.Square), deps=[t_raw] + sq.wdeps())
            sq.ready, sq.readers = t_sq, []
            raw.readers.append(t_sq)
            acc.readers += [t_raw]
            lat = env["lat"]
            t_ss = P.op("pe", lambda e: e.matmul(lat.t[:], lhsT=ones, rhs=sq.t[:], start=(j == 0), stop=(j == nj - 1)),
                        deps=[t_sq, t_c] + (lat.wdeps() if j == 0 else []))
            sq.readers.append(t_ss)
            if j < nj - 1:
                return
            lat.ready, lat.readers = t_ss, []
            rt, t_rc = rstd_tile(env, lat, 128, d, t_ss)
            for jj in range(nj):
                rw = env["raw"][jj]
                ob = nxt(env, "ob")
                t_a = P.op("dve", lambda e, rw=rw, ob=ob, jj=jj: e.scalar_tensor_tensor(out=ob.t[:], in0=rw.t[:], scalar=gvec[:, gcol0 + jj:gcol0 + jj + 1], in1=rt.t[:], op0=ALU.mult, op1=ALU.mult),
                           deps=[t_rc, rw.ready, t_gv] + ob.wdeps())
                rw.readers.append(t_a)
                rt.readers.append(t_a)
                ob.ready = t_a
                store(env, ob, 128, dsts[jj], t_a)

        def run_jobs(jobs, L1=1, L2=None):
            q1 = []
            stages = []

            def advance():
                nxt_stages = []
                if len(q1) > L1 or (drain[0] and q1):
                    p, r = q1.pop(0)
                    nxt_stages.append(p(r))
                else:
                    nxt_stages.append(None)
                for c in stages:
                    nxt_stages.append(c() if c is not None else None)
                while nxt_stages and nxt_stages[-1] is None:
                    nxt_stages.pop()
                stages[:] = nxt_stages
            drain = [False]
            for job in jobs:
                _JOBCNT[0] += 1
                if _JOBCNT[0] > JOBLIMIT:
                    break
                res = job[0]()
                q1.append((job[1], res))
                advance()
            drain[0] = True
            while q1 or stages:
                advance()

        with contextlib.ExitStack() as esA:
            def SA(name, shape, dt):
                return esA.enter_context(nc.sbuf_tensor(U(name), shape, dt))
            hnT = SA("hnT", [128, 16, T], BF16)

            with contextlib.ExitStack() as es1:
                def S1(name, shape, dt):
                    return es1.enter_context(nc.sbuf_tensor(U(name), shape, dt))

                def PS1(name, shape, dt):
                    return es1.enter_context(nc.psum_tensor(U(name), shape, dt))
                xr = [Slot(S1(f"xr{i}", [128, D], F32)) for i in range(3)]
                xn = [Slot(S1(f"xn{i}", [128, D], BF16)) for i in range(3)]
                junk = S1("junk", [128, D], BF16)
                gbc = S1("gbc", [128, D], F32)
                ssq1 = [S1(f"ssq1{i}", [128, 1], F32) for i in range(3)]
                rs1 = [Slot(S1(f"rs1{i}", [128, 1], F32)) for i in range(3)]
                tp = [[Slot(PS1(f"tp{i}{j}", [128, 8, 128], BF16)) for j in range(2)] for i in range(2)]
                t_g = P.dma("sp", "gbc", lambda e: e.dma_start(out=gbc[:], in_=gain_bc))
                CUT = 1280
                st1 = {}

                def stageA(tt):
                    s = tt % 3
                    X = xr[s]
                    t_x = P.dma("sp", f"x{s}", lambda e: e.dma_start(out=X.t[:], in_=x_perm[tt * 128:(tt + 1) * 128, :]), deps=X.wdeps())
                    X.ready, X.readers = t_x, []
                    t_sq = P.op("act", lambda e: e.activation(out=junk[:], in_=X.t[:], func=AF.Square, accum_out=ssq1[s][:]), deps=[t_x] + rs1[s].wdeps())
                    t_sr = P.op("act", lambda e: e.activation(out=rs1[s].t[:], in_=ssq1[s][:], func=AF.Sqrt, bias=epsb[:], scale=1.0 / D), deps=[t_sq, t_eps])
                    t_rc = P.op("dve", lambda e: e.reciprocal(out=rs1[s].t[:], in_=rs1[s].t[:]), deps=[t_sr])
                    t_xg = P.op("pool", lambda e: e.tensor_tensor(out=X.t[:, CUT:D], in0=X.t[:, CUT:D], in1=gbc[:, CUT:D], op=ALU.mult), deps=[t_sq, t_g, t_x])
                    rs1[s].ready, rs1[s].readers = t_rc, []
                    X.readers = [t_sq, t_xg]
                    st1[tt] = dict(t_x=t_x, t_rc=t_rc, t_xg=t_xg)

                def stageB(tt):
                    s = tt % 3
                    X, XN = xr[s], xn[s]
                    d = st1[tt]
                    t_xn = P.op("dve", lambda e: e.scalar_tensor_tensor(out=XN.t[:, 0:CUT], in0=X.t[:, 0:CUT], scalar=rs1[s].t[:, 0:1], in1=gbc[:, 0:CUT], op0=ALU.mult, op1=ALU.mult),
                                deps=[d["t_rc"], t_g, d["t_x"]] + XN.wdeps())
                    t_xn2 = P.op("act", lambda e: e.activation(out=XN.t[:, CUT:D], in_=X.t[:, CUT:D], func=AF.Copy, scale=rs1[s].t[:, 0:1]),
                                 deps=[d["t_xg"], d["t_rc"]] + XN.wdeps())
                    rs1[s].readers += [t_xn, t_xn2]
                    X.readers += [t_xn, t_xn2]
                    XN.ready, XN.readers = t_xn, []
                    toks = []
                    for j in range(2):
                        TP = tp[tt % 2][j]
                        tok = None
                        for cc in range(8):
                            ch = 8 * j + cc
                            tok = P.op("pe", lambda e, TP=TP, cc=cc, ch=ch: e.transpose(out=TP.t[:, cc, :], in_=XN.t[:, ch * 128:(ch + 1) * 128], identity=ident),
                                       deps=([t_xn, t_xn2, t_c] + TP.wdeps()) if cc == 0 else (), sig=(cc == 7))
                        TP.ready, TP.readers = tok, []
                        XN.readers.append(tok)
                        toks.append(tok)
                    d["toks"] = toks

                def stageC(tt):
                    d = st1[tt]
                    for j in range(2):
                        TP = tp[tt % 2][j]
                        tok = d["toks"][j]
                        if j == 0:
                            t_ev = P.op("act", lambda e, TP=TP: e.activation(out=hnT[:, 0:8, tt * 128:(tt + 1) * 128], in_=TP.t[:, :, :], func=AF.Copy), deps=[tok])
                        else:
                            t_ev = P.op("dve", lambda e, TP=TP: e.tensor_copy(out=hnT[:, 8:16, tt * 128:(tt + 1) * 128], in_=TP.t[:, :, :]), deps=[tok])
                        TP.readers.append(t_ev)

                for it in range(32 + 2):
                    if it < 32:
                        stageA(it)
                    if 0 <= it - 1 < 32:
                        stageB(it - 1)
                    if 0 <= it - 2 < 32:
                        stageC(it - 2)
                if DEBUG:
                    HNT = nc.dram_tensor("HNT", [128, 16, T], BF16, kind="ExternalOutput").ap()
                    P.barrier()
                    for c4 in range(4):
                        P.dma("sp", "dbg", lambda e, c4=c4: e.dma_start(out=HNT[:, 4 * c4:4 * c4 + 4, :], in_=hnT[:, 4 * c4:4 * c4 + 4, :]))
                P.barrier()
                P.emit()

            with contextlib.ExitStack() as es2:
              if STOP >= 2:
                def S2(name, shape, dt):
                    return es2.enter_context(nc.sbuf_tensor(U(name), shape, dt))

                def PS2(name, shape, dt):
                    return es2.enter_context(nc.psum_tensor(U(name), shape, dt))
                env = make_post_env(es2, S2, PS2)
                wr = [Slot(S2(f"wr{i}", [128, 16, 256], BF16)) for i in range(3)]
                wn = [0]

                def load_block(col0, ncol):
                    W = wr[wn[0] % 3]
                    i = wn[0] % 3
                    wn[0] += 1
                    tk = P.dma("pool", f"w{i}", lambda e: e.dma_start(out=W.t[:, :, 0:ncol], in_=w_in[:, col0:col0 + ncol].rearrange("(c p) n -> p c n", p=128)), deps=W.wdeps())
                    W.ready, W.readers = tk, []
                    return W

                def fm_job(W, coff, M, n, post):
                    def main():
                        lhs = [W.t[:, c, coff:coff + M] for c in range(16)]
                        rhs = [hnT[:, c, n * 512:(n + 1) * 512] for c in range(16)]
                        acc, tok = main_fm(env, lhs, rhs, M, [W.ready])
                        W.readers.append(tok)
                        return acc
                    return (main, post)

                def v_job(W, tile, half, ncol):
                    def main():
                        acc = nxt(env, "acc")
                        tok = None
                        for c in range(16):
                            tok = P.op("pe", lambda e, c=c, acc=acc: e.matmul(acc.t[:, 0:ncol], lhsT=hnT[:, c, tile * 128:(tile + 1) * 128], rhs=W.t[:, c, 0:ncol], start=(c == 0), stop=(c == 15)),
                                       deps=([W.ready] + acc.wdeps()) if c == 0 else (), sig=(c == 15))
                        acc.ready, acc.readers = tok, []
                        W.readers.append(tok)
                        return acc
                    return main

                sets = []
                def set_cq():
                    Ws = [load_block(C_CQ + 256 * b, 256) for b in range(3)]
                    jobs = []
                    for n in range(4):
                        dsts = [CQT[:, jj, n * 512:(n + 1) * 512] for jj in range(6)]
                        for j in range(6):
                            jobs.append(fm_job(Ws[j // 2], (j % 2) * 128, 128, n, lambda acc, j=j, dsts=dsts: post_latent(env, acc, j, 6, 0, 768, dsts)))
                    return jobs
                def set_ckv():
                    Ws = [load_block(C_CKV + 256 * b, 256) for b in range(2)]
                    Wk = load_block(C_KR, 64)
                    jobs = []
                    for n in range(8):
                        dsts = [CKVT[:, jj, n * 512:(n + 1) * 512] for jj in range(4)]
                        for j in range(4):
                            jobs.append(fm_job(Ws[j // 2], (j % 2) * 128, 128, n, lambda acc, j=j, dsts=dsts: post_latent(env, acc, j, 4, 6, 512, dsts)))
                        jobs.append(fm_job(Wk, 0, 64, n, lambda acc, n=n: post_head(env, acc, 64, 13, 64, True, n, KR[:, n * 512:(n + 1) * 512])))
                    return jobs

                def set_heads(col0, b, nchunks, gcol, dstT):
                    def f():
                        W = load_block(col0 + 256 * b, 256)
                        jobs = []
                        for n in range(nchunks):
                            for hh in range(2):
                                h = 2 * b + hh
                                jobs.append(fm_job(W, hh * 128, 128, n, lambda acc, n=n, h=h: post_head(env, acc, 128, gcol, 128, True, n, dstT[h, :, n * 512:(n + 1) * 512])))
                        return jobs
                    return f

                def set_gate(col0, b, hbase):
                    def f():
                        W = load_block(col0 + 256 * b, 256)
                        jobs = []
                        for n in range(4):
                            for hh in range(2):
                                h = hbase + 2 * b + hh
                                jobs.append(fm_job(W, hh * 128, 128, n, lambda acc, n=n, h=h: post_silu(env, acc, SG[h, :, n * 512:(n + 1) * 512])))
                        return jobs
                    return f

                def set_v(b):
                    def f():
                        W = load_block(C_VB + 256 * b, 256)
                        jobs = []
                        for tile in range(32):
                            jobs.append((v_job(W, tile, b, 256), lambda acc, tile=tile: post_copy(env, acc, VB[tile * 128:(tile + 1) * 128, b * 256:(b + 1) * 256], 256, "act" if tile % 2 == 0 else "dve")))
                        return jobs
                    return f

                sets.append(set_cq)
                sets.append(set_ckv)
                for b in range(4):
                    sets.append(set_heads(C_QB, b, 4, 14, QTB))
                for b in range(4):
                    sets.append(set_heads(C_KB, b, 8, 15, KTB))
                for b in range(4):
                    sets.append(set_v(b))
                for b in range(4):
                    sets.append(set_gate(C_GA, b, 0))
                for b in range(4):
                    sets.append(set_gate(C_GB, b, 8))
                alljobs = []
                for f in sets:
                    alljobs.append(f)
                alljobs = alljobs[:NSETS]
                built = alljobs[0]()
                for k in range(len(alljobs)):
                    cur = built
                    if k + 1 < len(alljobs) and k >= 2:
                        built = alljobs[k + 1]()
                        run_jobs(cur)
                    else:
                        run_jobs(cur)
                        if k + 1 < len(alljobs):
                            built = alljobs[k + 1]()
                P.barrier()
                P.emit()

        with contextlib.ExitStack() as es3:
          if STOP >= 3:
            def S3(name, shape, dt):
                return es3.enter_context(nc.sbuf_tensor(U(name), shape, dt))

            def PS3(name, shape, dt):
                return es3.enter_context(nc.psum_tensor(U(name), shape, dt))
            env = make_post_env(es3, S3, PS3, nacc=5, with_lat=False, nswp=1)
            ckvT = S3("ckvT", [128, 4, T], BF16)
            cqT = S3("cqT", [128, 6, TO], BF16)
            wk = S3("wk", [128, 4, 8, 128], BF16)
            wv = S3("wv", [128, 4, 1024], BF16)
            wqn = S3("wqn", [128, 6, 8, 128], BF16)
            wqr = S3("wqr", [128, 6, 512], BF16)
            ukv5 = w_ukv.rearrange("(kc p) (h two c) -> p kc h two c", p=128, two=2, c=128)
            uq_v = w_uq.rearrange("(kc p) (h c) -> p kc h c", p=128, c=192)
            tl = []
            for kc in range(4):
                tl.append(P.dma("pool", "wk", lambda e, kc=kc: e.dma_start(out=wk[:, kc, :, :], in_=ukv5[:, kc, :, 0, :])))
                tl.append(P.dma("pool", "wv", lambda e, kc=kc: e.dma_start(out=wv[:, kc, :].rearrange("p (h c) -> p h c", c=128), in_=ukv5[:, kc, :, 1, :])))
            for kc in range(6):
                tl.append(P.dma("pool", "wqn", lambda e, kc=kc: e.dma_start(out=wqn[:, kc, :, :], in_=uq_v[:, kc, :, 0:128])))
                tl.append(P.dma("pool", "wqr", lambda e, kc=kc: e.dma_start(out=wqr[:, kc, :].rearrange("p (h c) -> p h c", c=64), in_=uq_v[:, kc, :, 128:192])))
            for kc in range(4):
                tl.append(P.dma("sp", "ckv", lambda e, kc=kc: e.dma_start(out=ckvT[:, kc, :], in_=CKVT[:, kc, :])))
            for kc in range(6):
                tl.append(P.dma("sp", "cq", lambda e, kc=kc: e.dma_start(out=cqT[:, kc, :], in_=CQT[:, kc, :])))
            tl = last_tokens(tl)
            jobs = []

            def job_b(lhs_fn, rhs_fn, nk, M, post):
                def main():
                    acc, tok = main_fm(env, [lhs_fn(k) for k in range(nk)], [rhs_fn(k) for k in range(nk)], M, tl)
                    return acc
                return (main, post)
            jk, jqn, jqr, jv = [], [], [], []
            for h in range(8):
                for n in range(8):
                    jk.append(job_b(lambda k, h=h: wk[:, k, h, :], lambda k, n=n: ckvT[:, k, n * 512:(n + 1) * 512], 4, 128,
                                    lambda acc, h=h, n=n: post_head(env, acc, 128, 12, 128, None, n, KTA[h, :, n * 512:(n + 1) * 512])))
            for h in range(8):
                for n in range(4):
                    jqn.append(job_b(lambda k, h=h: wqn[:, k, h, :], lambda k, n=n: cqT[:, k, n * 512:(n + 1) * 512], 6, 128,
                                     lambda acc, h=h, n=n: post_head(env, acc, 128, 10, 128, None, n, QTA[h, :, n * 512:(n + 1) * 512])))
            for pr in range(4):
                for n in range(4):
                    jqr.append(job_b(lambda k, pr=pr: wqr[:, k, pr * 128:(pr + 1) * 128], lambda k, n=n: cqT[:, k, n * 512:(n + 1) * 512], 6, 128,
                                     lambda acc, pr=pr, n=n: post_head(env, acc, 128, 11, 64, True, n, QRA[pr, :, n * 512:(n + 1) * 512], ones_m=BD1, R_m=BDR, tkind=64)))
            for half in range(2):
                for tile in range(32):
                    def main(half=half, tile=tile):
                        acc = nxt(env, "acc")
                        tok = None
                        for k in range(4):
                            tok = P.op("pe", lambda e, k=k, acc=acc: e.matmul(acc.t[:, :], lhsT=ckvT[:, k, tile * 128:(tile + 1) * 128], rhs=wv[:, k, half * 512:(half + 1) * 512], start=(k == 0), stop=(k == 3)),
                                       deps=(tl + acc.wdeps()) if k == 0 else (), sig=(k == 3))
                        acc.ready, acc.readers = tok, []
                        return acc
                    jv.append((main, lambda acc, half=half, tile=tile: post_copy(env, acc, VA[tile * 128:(tile + 1) * 128, half * 512:(half + 1) * 512], 512, "dve")))
            for i in range(16):
                jobs += [jk[4 * i], jv[4 * i], jqn[2 * i], jk[4 * i + 1], jv[4 * i + 1], jqr[i],
                         jk[4 * i + 2], jv[4 * i + 2], jqn[2 * i + 1], jk[4 * i + 3], jv[4 * i + 3]]
            env["alt_rstd"] = False
            run_jobs(jobs, L1=2)
            P.barrier()
            P.emit()

        with contextlib.ExitStack() as esY:
          if STOP >= 4:
            def SY(name, shape, dt):
                return esY.enter_context(nc.sbuf_tensor(U(name), shape, dt))
            Y = SY("Y", [128, 16, TO], BF16)
            ssqacc = SY("ssqacc", [128, 2, 16], F32)
            t_z = P.op("dve", lambda e: e.memset(ssqacc[:], 0.0))
            wo = [Slot(SY(f"wo{i}", [128, 16, 512], BF16)) for i in range(3)]

            def load_wo(cb):
                W = wo[cb % 3]
                i = cb % 3
                toks = []
                dp = W.wdeps()
                for half in range(2):
                    toks.append(P.dma("pool", f"wo{i}", lambda e, half=half: e.dma_start(out=W.t[:, 8 * half:8 * half + 8, :], in_=w_out[half * 1024:(half + 1) * 1024, cb * 512:(cb + 1) * 512].rearrange("(c p) n -> p c n", p=128)), deps=dp))
                W.ready, W.readers = toks[-1], []
                return W
            Wlist = [load_wo(cb) for cb in range(3)]

            with contextlib.ExitStack() as es4:
                def S4(name, shape, dt):
                    return es4.enter_context(nc.sbuf_tensor(U(name), shape, dt))

                def PS4(name, shape, dt):
                    return es4.enter_context(nc.psum_tensor(U(name), shape, dt))
                KT = [Slot(S4(f"KT{i}", [128, T], BF16)) for i in range(2)]
                VV = [Slot(S4(f"VV{i}", [128, 32, 128], BF16)) for i in range(2)]
                QT = [Slot(S4(f"QT{i}", [128, TO], BF16)) for i in range(2)]
                QR = [Slot(S4(f"QR{i}", [128, TO], BF16)) for i in range(2)]
                SGs = [Slot(S4(f"SGs{i}", [128, TO], BF16)) for i in range(2)]
                KRs = S4("KRs", [128, T], BF16)
                tabw = S4("tabw", [128, 2, 1536], BF16)
                tabm = S4("tabm", [128, 2, 128], BF16)
                pt = [Slot(S4(f"pt{i}", [128, 512], BF16)) for i in range(6)]
                rden = S4("rden", [128, 512], F32)
                osb = S4("osb", [128, 512], F32)
                osq = S4("osq", [128, 512], BF16)
                sps = [Slot(PS4(f"sps{i}", [128, 512], F32)) for i in range(4)]
                ops_ = [Slot(PS4(f"ops{i}", [128, 512], F32)) for i in range(2)]
                dps = [Slot(PS4(f"dps{i}", [128, 512], F32)) for i in range(1)]
                ssp = Slot(PS4("ssp", [128, 4], F32))
                t_kr0 = P.dma("sp", "krs", lambda e: e.dma_start(out=KRs[0:64, :], in_=KR))
                t_kz = P.op("pool", lambda e: e.memset(KRs[64:128, :], 0.0))
                for qs in QR:
                    qs.ready = P.op("pool", lambda e, qs=qs: e.memset(qs.t[64:128, :], 0.0))
                t_qz = qs.ready
                t_kr = t_kr0
                t_tw = [P.dma("sp", "tw", lambda e, oo=oo: e.dma_start(out=tabw[:, oo, :], in_=tabw_d[oo])) for oo in range(2)]
                t_tm = [P.dma("sp", "tm", lambda e, oo=oo: e.dma_start(out=tabm[:, oo, :], in_=tabm_d[oo])) for oo in range(2)]
                tab_tok = last_tokens(t_tw + t_tm)

                def load_head(H):
                    s = H % 2
                    mla = H < 8
                    h = H % 8
                    ktd = KTA if mla else KTB
                    vd = VA if mla else VB
                    qd = QTA if mla else QTB
                    k_, v_, q_, g_ = KT[s], VV[s], QT[s], SGs[s]
                    k_.ready = P.dma("sp", f"hk{s}", lambda e: e.dma_start(out=k_.t[:], in_=ktd[h]), deps=k_.wdeps())
                    k_.readers = []
                    vtoks = []
                    vdeps = v_.wdeps()
                    for q4 in range(4):
                        vtoks.append(P.dma("sp", f"hv{s}", lambda e, q4=q4: e.dma_start(out=v_.t[:, 8 * q4:8 * q4 + 8, :], in_=vd[q4 * 1024:(q4 + 1) * 1024, h * 128:(h + 1) * 128].rearrange("(kt p) c -> p kt c", p=128)),
                                           deps=vdeps))
                    v_.ready, v_.readers = vtoks[-1], []
                    q_.ready = P.dma("sp", f"hq{s}", lambda e: e.dma_start(out=q_.t[:], in_=qd[h]), deps=q_.wdeps())
                    q_.readers = []
                    g_.ready = P.dma("sp", f"hg{s}", lambda e: e.dma_start(out=g_.t[:], in_=SG[H]), deps=g_.wdeps())
                    g_.readers = []
                    if mla:
                        r_ = QR[s]
                        r_.ready = P.dma("sp", f"hr{s}", lambda e: e.dma_start(out=r_.t[0:64, :], in_=QRA[h // 2, (h % 2) * 64:(h % 2) * 64 + 64, :]), deps=r_.wdeps())
                        r_.readers = []

                cnt = {"s": 0, "p": 0, "o": 0, "mk": 0}

                pend = {"fin": None, "tk": None, "ty": None, "to": None, "tq": None}

                def do_chunk(H, qc, mla, k_, v_, q_, g_, r_, scale, gcol):
                    i_lo = 0 if mla else max(0, 4 * qc - 8)
                    tiles = []
                    for i in range(i_lo, 4 * qc + 4):
                        for oo in range(2):
                            r = i - 4 * qc
                            n0 = max(0, 2 * r) * 64
                            n1 = 512 if mla else min(8, 2 * r + 18) * 64
                            tiles.append((oo, i, r, n0, n1))
                    O = ops_[cnt["o"] % 2]
                    Dn = dps[0]
                    cnt["o"] += 1
                    nt = len(tiles)
                    sinfo = {}
                    q0 = qc * 512

                    def rec_qk(j):
                        oo, i, r, n0, n1 = tiles[j]
                        Sp = sps[cnt["s"] % 4]
                        cnt["s"] += 1
                        kcol = (0 if oo == 0 else TO) + i * 128
                        deps = [k_.ready, q_.ready] + Sp.wdeps()
                        if mla:
                            deps += [r_.ready, t_kr, t_kz, t_qz]
                            P.op("pe", lambda e: e.matmul(Sp.t[:, n0:n1], lhsT=k_.t[:, kcol:kcol + 128], rhs=q_.t[:, q0 + n0:q0 + n1], start=True, stop=False), deps=deps, sig=False)
                            tok = P.op("pe", lambda e: e.matmul(Sp.t[:, n0:n1], lhsT=KRs[:, kcol:kcol + 128], rhs=r_.t[:, q0 + n0:q0 + n1], start=False, stop=True))
                        else:
                            tok = P.op("pe", lambda e: e.matmul(Sp.t[:, n0:n1], lhsT=k_.t[:, kcol:kcol + 128], rhs=q_.t[:, q0 + n0:q0 + n1], start=True, stop=True), deps=deps)
                        Sp.ready, Sp.readers = tok, []
                        Pt = pt[cnt["p"] % 6]
                        cnt["p"] += 1
                        t_e = P.op("act", lambda e: e.activation(out=Pt.t[:, n0:n1], in_=Sp.t[:, n0:n1], func=AF.Exp, scale=float(scale)), deps=[tok] + Pt.wdeps())
                        Sp.readers.append(t_e)
                        Pt.ready, Pt.readers = t_e, []
                        if mla and r >= 0:
                            eng = "dve"
                            cnt["mk"] += 1
                            t_m = P.op(eng, lambda e: e.tensor_tensor(out=Pt.t[:, n0:n0 + 128], in0=Pt.t[:, n0:n0 + 128], in1=tabm[:, oo, :], op=ALU.mult), deps=[t_e] + tab_tok)
                            Pt.ready = t_m
                        elif not mla:
                            eng = "dve"
                            cnt["mk"] += 1
                            e0 = (n0 // 64 - 2 * r) * 64
                            t_m = P.op(eng, lambda e: e.tensor_tensor(out=Pt.t[:, n0:n1], in0=Pt.t[:, n0:n1], in1=tabw[:, oo, e0:e0 + (n1 - n0)], op=ALU.mult), deps=[t_e] + tab_tok)
                            Pt.ready = t_m
                        sinfo[j] = Pt

                    def rec_pv(j):
                        oo, i, r, n0, n1 = tiles[j]
                        Pt = sinfo[j]
                        kt = (0 if oo == 0 else 16) + i
                        deps = [Pt.ready, v_.ready, t_c] + ((O.wdeps() + Dn.wdeps()) if j == 0 else [])
                        P.op("pe", lambda e: e.matmul(O.t[:, n0:n1], lhsT=v_.t[:, kt, :], rhs=Pt.t[:, n0:n1], start=(j == 0), stop=(j == nt - 1), skip_group_check=True), deps=deps, sig=False)
                        tok = P.op("pe", lambda e: e.matmul(Dn.t[:, n0:n1], lhsT=ones, rhs=Pt.t[:, n0:n1], start=(j == 0), stop=(j == nt - 1), skip_group_check=True))
                        Pt.readers.append(tok)
                        return tok

                    for j0 in range(min(3, nt)):
                        rec_qk(j0)
                    last = None
                    for j in range(nt):
                        if j + 3 < nt:
                            rec_qk(j + 3)
                        last = rec_pv(j)
                        if j == 5 and pend["fin"] is not None:
                            pend["fin"]()
                            pend["fin"] = None
                    O.ready, Dn.ready = last, last
                    O.readers, Dn.readers = [], []
                    t_ln = P.op("act", lambda e: e.activation(out=rden[:], in_=Dn.t[:], func=AF.Ln), deps=[last, pend["to"]])
                    t_rd = P.op("act", lambda e: e.activation(out=rden[:], in_=rden[:], func=AF.Exp, scale=-1.0), deps=[t_ln])
                    t_o = P.op("dve", lambda e: e.tensor_tensor(out=osb[:], in0=O.t[:], in1=rden[:], op=ALU.mult), deps=[t_rd, last, pend["ty"], pend["tq"]])
                    O.readers.append(t_o)
                    Dn.readers.append(t_ln)
                    pend["to"] = t_o
                    k_.readers.append(last)
                    v_.readers.append(last)
                    q_.readers.append(last)
                    if mla:
                        r_.readers.append(last)

                    def fin_b():
                        t_q = P.op("act", lambda e: e.activation(out=osq[:], in_=osb[:], func=AF.Square), deps=[t_o, pend["tk"]])
                        pend["tq"] = t_q
                        tk = None
                        for tt in range(4):
                            tk = P.op("pe", lambda e, tt=tt: e.matmul(ssp.t[:, tt:tt + 1], lhsT=osq[:, tt * 128:(tt + 1) * 128], rhs=ones[:, 0:1], start=True, stop=True),
                                      deps=([t_q, t_c] + ssp.wdeps()) if tt == 0 else (), sig=(tt == 3))
                        mx = 0 if mla else 1
                        t_ac = P.op("dve", lambda e: e.tensor_tensor(out=ssqacc[:, mx, qc * 4:(qc + 1) * 4], in0=ssqacc[:, mx, qc * 4:(qc + 1) * 4], in1=ssp.t[:, 0:4], op=ALU.add), deps=[tk, t_z])
                        ssp.ready, ssp.readers = tk, [t_ac]
                        t_y = P.op("dve", lambda e: e.scalar_tensor_tensor(out=Y[:, H, q0:q0 + 512], in0=osb[:], scalar=gvec[:, gcol:gcol + 1], in1=g_.t[:, q0:q0 + 512], op0=ALU.mult, op1=ALU.mult),
                                   deps=[t_o, g_.ready, t_gv, t_q, tk])
                        pend["tk"], pend["ty"] = tk, t_y
                        g_.readers.append(t_y)
                    pend["fin"] = fin_b
                    pend["g"] = g_

                load_head(0)
                for H in range(16):
                    if pend["fin"] is not None:
                        pend["fin"]()
                        pend["fin"] = None
                    if H + 1 < 16:
                        load_head(H + 1)
                    s = H % 2
                    mla = H < 8
                    scale = 1.0 / np.sqrt(192.0) if mla else 1.0 / np.sqrt(128.0)
                    gcol = (16 if mla else 24) + (H % 8)
                    for qc in range(4):
                        do_chunk(H, qc, mla, KT[s], VV[s], QT[s], SGs[s], QR[s], scale, gcol)
                if pend["fin"] is not None:
                    pend["fin"]()
                P.barrier()
                P.emit()

            with contextlib.ExitStack() as es5:
              if STOP >= 5:
                def S5(name, shape, dt):
                    return es5.enter_context(nc.sbuf_tensor(U(name), shape, dt))

                def PS5(name, shape, dt):
                    return es5.enter_context(nc.psum_tensor(U(name), shape, dt))
                xo = [Slot(S5(f"xo{i}", [128, 512], F32)) for i in range(3)]
                t1 = [Slot(S5(f"t1{i}", [128, 512], F32)) for i in range(2)]
                ot = [Slot(S5(f"ot{i}", [128, 512], F32)) for i in range(3)]
                rsm = S5("rsm", [128, 2, 16], F32)
                pa = [Slot(PS5(f"pa{i}", [128, 512], F32)) for i in range(2)]
                pb = [Slot(PS5(f"pb{i}", [128, 512], F32)) for i in range(2)]
                t_r1 = P.op("act", lambda e: e.activation(out=rsm[:], in_=ssqacc[:], func=AF.Sqrt, bias=epsb[:], scale=1.0 / 1024.0), deps=[t_eps])
                t_r2 = P.op("dve", lambda e: e.reciprocal(out=rsm[:], in_=rsm[:]), deps=[t_r1])

                k = 0
                for cb in range(4):
                    Wc = Wlist[cb]
                    for tile in range(16):
                        X = xo[k % 3]
                        xi = k % 3
                        X.ready = P.dma("act", f"xo{xi}", lambda e, X=X, tile=tile, cb=cb: e.dma_start(out=X.t[:], in_=x_perm[tile * 128:(tile + 1) * 128, cb * 512:(cb + 1) * 512]), deps=X.wdeps())
                        X.readers = []
                        A, B = pa[k % 2], pb[k % 2]
                        tokA = tokB = None
                        for hh in range(8):
                            tokA = P.op("pe", lambda e, hh=hh, A=A, tile=tile, Wc=Wc: e.matmul(A.t[:], lhsT=Y[:, hh, tile * 128:(tile + 1) * 128], rhs=Wc.t[:, hh, :], start=(hh == 0), stop=(hh == 7)),
                                        deps=([Wc.ready] + A.wdeps()) if hh == 0 else (), sig=(hh == 7))
                        for hh in range(8):
                            tokB = P.op("pe", lambda e, hh=hh, B=B, tile=tile, Wc=Wc: e.matmul(B.t[:], lhsT=Y[:, 8 + hh, tile * 128:(tile + 1) * 128], rhs=Wc.t[:, 8 + hh, :], start=(hh == 0), stop=(hh == 7)),
                                        deps=B.wdeps() if hh == 0 else (), sig=(hh == 7))
                        Wc.readers.append(tokB)
                        A.ready, B.ready = tokA, tokB
                        T1 = t1[k % 2]
                        OT = ot[k % 3]
                        oi = k % 3
                        t_a = P.op("dve", lambda e, A=A, X=X, T1=T1, tile=tile: e.scalar_tensor_tensor(out=T1.t[:], in0=A.t[:], scalar=rsm[:, 0, tile:tile + 1], in1=X.t[:], op0=ALU.mult, op1=ALU.add),
                                   deps=[tokA, X.ready, t_r2] + T1.wdeps())
                        A.readers = [t_a]
                        X.readers = [t_a]
                        T1.ready, T1.readers = t_a, []
                        t_b = P.op("dve", lambda e, B=B, T1=T1, OT=OT, tile=tile: e.scalar_tensor_tensor(out=OT.t[:], in0=B.t[:], scalar=rsm[:, 1, tile:tile + 1], in1=T1.t[:], op0=ALU.mult, op1=ALU.add),
                                   deps=[tokB, t_a] + OT.wdeps())
                        B.readers = [t_b]
                        T1.readers = [t_b]
                        t_st = P.dma("sp", f"ot{oi}", lambda e, OT=OT, tile=tile, cb=cb: e.dma_start(out=out_own[tile * 128:(tile + 1) * 128, cb * 512:(cb + 1) * 512], in_=OT.t[:]), deps=[t_b])
                        OT.ready, OT.readers = None, [t_st]
                        final_tokens.append(t_st)
                        k += 1
                    if cb == 0:
                        Wlist.append(load_wo(3))
                P.barrier()
                P.emit()
    return nc


def _rope_tables(pos, d):
    half = d // 2
    inv = (np.float32(10000.0) ** (-np.arange(0, d, 2, dtype=np.float32) / np.float32(d))).astype(np.float32)
    ang = (pos.astype(np.float32)[None, :] * inv[:, None]).astype(np.float32)
    c = np.cos(ang).astype(np.float32)
    s = np.sin(ang).astype(np.float32)
    return np.concatenate([c, c], 0), np.concatenate([-s, s], 0)


def _mask_tables(c):
    kk = np.arange(64)[:, None]
    n = np.arange(64)[None, :]
    tabw = np.zeros((2, 128, 24 * 64), np.float32)
    tabm = np.zeros((2, 128, 128), np.float32)
    for oo in range(2):
        half_k = c if oo == 0 else 1 - c
        for s in range(2):
            for E in range(24):
                delta = E - s
                diff = 128 * delta + 64 * (c - half_k) + (n - kk)
                ge = diff >= 0
                w = (ge & (diff <= 128)).astype(np.float32) + (ge & (diff % 4 == 0) & (diff <= 512)) + (ge & (diff % 16 == 0) & (diff <= 2048))
                tabw[oo, s * 64:(s + 1) * 64, E * 64:(E + 1) * 64] = w
                if E < 2:
                    tabm[oo, s * 64:(s + 1) * 64, E * 64:(E + 1) * 64] = ge.astype(np.float32)
    return tabw.astype(NPBF), tabm.astype(NPBF)


def _perm(c):
    jb = np.arange(32)[:, None]
    i = np.arange(64)[None, :]
    own = (128 * jb + 64 * c + i).reshape(-1)
    oth = (128 * jb + 64 * (1 - c) + i).reshape(-1)
    return np.concatenate([own, oth])


def _consts():
    cst = np.zeros((128, 6, 128), np.float32)
    cst[0:64, 4, 0:64] = 1.0
    cst[64:128, 4, 64:128] = 1.0
    qq = np.arange(64)
    cst[qq, 5, (qq + 32) % 64] = 1.0
    cst[64 + qq, 5, 64 + (qq + 32) % 64] = 1.0
    cst[:, 0, :] = np.eye(128)
    cst[:, 1, :] = 1.0
    p = np.arange(128)
    cst[p, 2, (p + 64) % 128] = 1.0
    q = np.arange(64)
    cst[q, 3, (q + 32) % 64] = 1.0
    return cst.astype(NPBF)


def _gvec(qa, kva, gq, gk, dq, dk, mo, do):
    g = np.zeros((128, 32), np.float32)
    g[:, 0:6] = qa.reshape(6, 128).T
    g[:, 6:10] = kva.reshape(4, 128).T
    g[:, 10] = gq[:128]
    g[0:64, 11] = gq[128:]
    g[64:128, 11] = gq[128:]
    g[:, 12] = gk[:128]
    g[0:64, 13] = gk[128:]
    g[:, 14] = dq
    g[:, 15] = dk
    g[:, 16:24] = mo.reshape(8, 128).T
    g[:, 24:32] = do.reshape(8, 128).T
    return g


_NC_CACHE = {}


def make_in_maps(x, norm_gain, w_in, q_a_norm_gain, kv_a_norm_gain, w_uq, w_ukv,
                 mla_q_norm_gain, mla_k_norm_gain, dil_q_norm_gain, dil_k_norm_gain,
                 mla_out_norm_gain, dil_out_norm_gain, w_out):
    f = lambda a: np.ascontiguousarray(np.asarray(a, dtype=np.float32))
    x = f(x)
    w_in0, w_uq0, w_ukv0, w_out0 = f(w_in)[0], f(w_uq)[0], f(w_ukv)[0], f(w_out)[0]
    gain_bc = np.ascontiguousarray(np.broadcast_to(f(norm_gain)[0][None, :], (128, D)))
    gv = _gvec(f(q_a_norm_gain)[0], f(kv_a_norm_gain)[0], f(mla_q_norm_gain)[0], f(mla_k_norm_gain)[0],
               f(dil_q_norm_gain)[0], f(dil_k_norm_gain)[0], f(mla_out_norm_gain)[0], f(dil_out_norm_gain)[0])
    cst = _consts()
    in_maps = []
    perms = []
    for core in range(8):
        b, c = core // 2, core % 2
        pm = _perm(c)
        perms.append(pm)
        cs128, sn128 = _rope_tables(pm, 128)
        cs64, sn64 = _rope_tables(pm, 64)
        cs64 = np.ascontiguousarray(np.concatenate([cs64, cs64], 0))
        sn64 = np.ascontiguousarray(np.concatenate([sn64, sn64], 0))
        tabw, tabm = _mask_tables(c)
        in_maps.append({
            "x_perm": np.ascontiguousarray(x[b][pm]),
            "w_in": w_in0, "w_uq": w_uq0, "w_ukv": w_ukv0, "w_out": w_out0,
            "gain_bc": gain_bc, "gvec": gv,
            "cs128": cs128, "sn128": sn128, "cs64": cs64, "sn64": sn64,
            "tabw": tabw, "tabm": tabm, "consts": cst,
        })
    return in_maps, perms


def kernel(**inputs):
    in_maps, perms = make_in_maps(**inputs)
    if "nc" not in _NC_CACHE:
        _NC_CACHE["nc"] = build_program()
    nc = _NC_CACHE["nc"]
    res = run_bass_kernel_spmd(nc, in_maps, core_ids=list(range(8)))
    out = np.empty((4, T, D), np.float32)
    for core in range(8):
        b = core // 2
        out[b][perms[core][:TO]] = np.asarray(res.results[core]["out_own"], dtype=np.float32)
    return out
```
